# Optimizing a Trainium2 kernel written in Bass

```python
import math
import jax, jax.numpy as jnp
from jax import lax
import numpy as np

D_MODEL = 1024
BATCH = 8
SEQ = 4096
DEPTH = 1

CHUNK = 64
MIX_W = D_MODEL
FOX_W = MIX_W // 2
LRU_W = MIX_W - FOX_W
FOX_HEADS = 8
FOX_HD = FOX_W // FOX_HEADS
LRU_BLOCKS = 8
LRU_BW = LRU_W // LRU_BLOCKS
LRU_C = 8.0
CONV_K = 4
D_FF = 4 * D_MODEL
Q_BLOCK = 128
LN_EPS = 1e-5
DN_ALPHA = (2.0 * DEPTH) ** 0.25
DN_BETA = (8.0 * DEPTH) ** -0.25

Q_OFF = 0
K_OFF = Q_OFF + FOX_W
V_OFF = K_OFF + FOX_W
LX_OFF = V_OFF + FOX_W
LG_OFF = LX_OFF + LRU_W
FG_OFF = LG_OFF + LRU_W
IN_COLS = FG_OFF + FOX_HEADS

kernel_name = "fox_rglru_macaron_deepnorm_block"


def layer_norm(x, g, b):
    xf = x.astype(jnp.float32)
    mu = jnp.mean(xf, axis=-1, keepdims=True)
    var = jnp.mean(jnp.square(xf - mu), axis=-1, keepdims=True)
    y = (xf - mu) * lax.rsqrt(var + LN_EPS) * g.astype(jnp.float32) + b.astype(jnp.float32)
    return y.astype(x.dtype)


def swiglu(x, w_gate, w_up, w_down):
    return (jax.nn.silu(x @ w_gate) * (x @ w_up)) @ w_down


def forgetting_attention(q, k, v, fg_logit):
    seq = q.shape[1]
    scale = 1.0 / math.sqrt(FOX_HD)
    cum = jnp.cumsum(jax.nn.log_sigmoid(fg_logit.astype(jnp.float32)), axis=1)
    cum = cum.transpose(0, 2, 1)
    qh = q.transpose(0, 2, 1, 3)
    kh = k.transpose(0, 2, 1, 3)
    vh = v.transpose(0, 2, 1, 3)
    outs = []
    for i in range(seq // Q_BLOCK):
        q0, q1 = i * Q_BLOCK, (i + 1) * Q_BLOCK
        s = jnp.einsum('bhqd,bhkd->bhqk', qh[:, :, q0:q1], kh[:, :, :q1],
                       preferred_element_type=jnp.float32) * scale
        s = s + cum[:, :, q0:q1, None] - cum[:, :, None, :q1]
        mask = jnp.arange(q0, q1)[:, None] >= jnp.arange(q1)[None, :]
        s = jnp.where(mask, s, -1e30)
        p = jax.nn.softmax(s, axis=-1).astype(vh.dtype)
        outs.append(jnp.einsum('bhqk,bhkd->bqhd', p, vh[:, :, :q1]))
    return jnp.concatenate(outs, axis=1)


def causal_depthwise_conv(u, w, b):
    y = lax.conv_general_dilated(u, w[:, None, :], window_strides=(1,), padding=[(CONV_K - 1, 0)],
                                 dimension_numbers=('NWC', 'WIO', 'NWC'),
                                 feature_group_count=u.shape[-1])
    return y + b


def _lin_rec_combine(c1, c2):
    a1, b1 = c1
    a2, b2 = c2
    return a1 * a2, a2 * b1 + b2


def rg_lru(u, wa, ba, wx, bx, lam):
    bsz, seq, width = u.shape
    ub = u.reshape(bsz, seq, LRU_BLOCKS, LRU_BW)
    r = jax.nn.sigmoid(jnp.einsum('bshi,hij->bshj', ub, wa) + ba).reshape(bsz, seq, width)
    gi = jax.nn.sigmoid(jnp.einsum('bshi,hij->bshj', ub, wx) + bx).reshape(bsz, seq, width)
    log_a = -LRU_C * r.astype(jnp.float32) * jax.nn.softplus(-lam.astype(jnp.float32))
    a = jnp.exp(log_a)
    bterm = jnp.sqrt(-jnp.expm1(2.0 * log_a)) * (gi * u).astype(jnp.float32)
    _, h = lax.associative_scan(_lin_rec_combine, (a, bterm), axis=1)
    return h.astype(u.dtype)


def setup_inputs(seed: int = 0) -> dict:
    key = jax.random.key(seed)
    ks = iter(jax.random.split(key, 32))
    f32 = jnp.float32

    def nrm(shape, scale):
        return jax.random.normal(next(ks), shape, f32) * scale

    d_in, d_ff = D_MODEL ** -0.5, D_FF ** -0.5
    x = jax.random.normal(next(ks), (BATCH, SEQ, D_MODEL), f32)
    w_in = nrm((DEPTH, D_MODEL, IN_COLS), d_in)
    w_in = w_in.at[:, :, V_OFF:V_OFF + FOX_W].multiply(DN_BETA)
    a0 = jax.random.uniform(next(ks), (DEPTH, LRU_W), f32, 0.9, 0.999)
    p = a0 ** (1.0 / LRU_C)
    lru_lambda = jnp.log(p) - jnp.log1p(-p)
    return {
        "x": x,
        "ffn1_w_gate": nrm((DEPTH, D_MODEL, D_FF), d_in),
        "ffn1_w_up": nrm((DEPTH, D_MODEL, D_FF), d_in),
        "ffn1_w_down": nrm((DEPTH, D_FF, D_MODEL), d_ff * DN_BETA),
        "ln1_g": 1.0 + nrm((DEPTH, D_MODEL), 0.02),
        "ln1_b": nrm((DEPTH, D_MODEL), 0.02),
        "w_in": w_in,
        "b_forget": 3.0 + nrm((DEPTH, FOX_HEADS), 0.1),
        "conv_w": nrm((DEPTH, CONV_K, LRU_W), CONV_K ** -0.5),
        "conv_b": nrm((DEPTH, LRU_W), 0.02),
        "rg_wa": nrm((DEPTH, LRU_BLOCKS, LRU_BW, LRU_BW), LRU_BW ** -0.5),
        "rg_ba": nrm((DEPTH, LRU_BLOCKS, LRU_BW), 0.02),
        "rg_wx": nrm((DEPTH, LRU_BLOCKS, LRU_BW, LRU_BW), LRU_BW ** -0.5),
        "rg_bx": nrm((DEPTH, LRU_BLOCKS, LRU_BW), 0.02),
        "lru_lambda": lru_lambda,
        "w_out": nrm((DEPTH, MIX_W, D_MODEL), (MIX_W ** -0.5) * DN_BETA),
        "ln2_g": 1.0 + nrm((DEPTH, D_MODEL), 0.02),
        "ln2_b": nrm((DEPTH, D_MODEL), 0.02),
        "ffn2_w_gate": nrm((DEPTH, D_MODEL, D_FF), d_in),
        "ffn2_w_up": nrm((DEPTH, D_MODEL, D_FF), d_in),
        "ffn2_w_down": nrm((DEPTH, D_FF, D_MODEL), d_ff * DN_BETA),
        "ln3_g": 1.0 + nrm((DEPTH, D_MODEL), 0.02),
        "ln3_b": nrm((DEPTH, D_MODEL), 0.02),
    }


def reference(x, ffn1_w_gate, ffn1_w_up, ffn1_w_down, ln1_g, ln1_b, w_in, b_forget,
              conv_w, conv_b, rg_wa, rg_ba, rg_wx, rg_bx, lru_lambda, w_out,
              ln2_g, ln2_b, ffn2_w_gate, ffn2_w_up, ffn2_w_down, ln3_g, ln3_b):
    bsz, seq, _ = x.shape
    for l in range(DEPTH):
        x = layer_norm(DN_ALPHA * x + 0.5 * swiglu(x, ffn1_w_gate[l], ffn1_w_up[l], ffn1_w_down[l]),
                       ln1_g[l], ln1_b[l])
        z = x @ w_in[l]
        q = z[..., Q_OFF:Q_OFF + FOX_W].reshape(bsz, seq, FOX_HEADS, FOX_HD)
        k = z[..., K_OFF:K_OFF + FOX_W].reshape(bsz, seq, FOX_HEADS, FOX_HD)
        v = z[..., V_OFF:V_OFF + FOX_W].reshape(bsz, seq, FOX_HEADS, FOX_HD)
        fg = z[..., FG_OFF:FG_OFF + FOX_HEADS] + b_forget[l]
        fox = forgetting_attention(q, k, v, fg).reshape(bsz, seq, FOX_W)
        u = causal_depthwise_conv(z[..., LX_OFF:LX_OFF + LRU_W], conv_w[l], conv_b[l])
        rec = rg_lru(u, rg_wa[l], rg_ba[l], rg_wx[l], rg_bx[l], lru_lambda[l])
        lru = jax.nn.gelu(z[..., LG_OFF:LG_OFF + LRU_W]) * rec
        mix = jnp.concatenate([fox, lru], axis=-1) @ w_out[l]
        x = layer_norm(DN_ALPHA * x + mix, ln2_g[l], ln2_b[l])
        x = layer_norm(DN_ALPHA * x + 0.5 * swiglu(x, ffn2_w_gate[l], ffn2_w_up[l], ffn2_w_down[l]),
                       ln3_g[l], ln3_b[l])
    return x
```

```python
import math
import numpy as np
from contextlib import ExitStack
import concourse.bass as bass
import concourse.mybir as mybir
from concourse.bass_utils import run_bass_kernel_spmd

F32 = mybir.dt.float32
BF16 = mybir.dt.bfloat16
AF = mybir.ActivationFunctionType
ALU = mybir.AluOpType

D = 1024
S = 4096
DFF = 4096
T = 512
NT = S // T
NCH = S // 128
IN_COLS = 2568
ALPHA = 2.0 ** 0.25
LN_EPS = 1e-5
EPS_P = LN_EPS / (ALPHA * ALPHA)
NSLOT = 6
NSTREAM = 110


class Op:
    __slots__ = ("eng", "fn", "deps", "chan", "marked", "val", "idx")


class Prog:
    def __init__(self, nc, same_engine_sync=True):
        self.nc = nc
        self.ops = []
        self.last_writer = {}
        self.readers = {}
        self.chan_n = {}
        self.chan_bulk = {}
        self.same_engine_sync = same_engine_sync

    def add(self, eng, fn, reads=(), writes=(), chan=None, bulk=False):
        op = Op()
        op.eng = eng
        op.fn = fn
        op.chan = chan
        op.marked = False
        op.val = None
        op.idx = len(self.ops)
        deps = {}
        wset = set(writes)
        for r in reads:
            w = self.last_writer.get(r)
            if w is not None:
                deps[w.idx] = w
        for r in wset:
            w = self.last_writer.get(r)
            if w is not None:
                deps[w.idx] = w
            rd = self.readers.get(r)
            if rd:
                for o in rd.values():
                    deps[o.idx] = o
        for r in wset:
            self.last_writer[r] = op
            self.readers[r] = {}
        for r in reads:
            if r in wset:
                continue
            d = self.readers.setdefault(r, {})
            if chan is None:
                d[eng] = op
            else:
                d[("dma", op.idx)] = op
        deps.pop(op.idx, None)
        op.deps = list(deps.values())
        if chan is not None:
            self.chan_n[chan] = self.chan_n.get(chan, 0) + 1
            op.val = 16 * self.chan_n[chan]
            if bulk:
                self.chan_bulk[chan] = True
        self.ops.append(op)
        return op

    def _needs_wait(self, op, d):
        if d.chan is not None:
            if op.chan == d.chan and self.chan_bulk.get(d.chan):
                return False
            return True
        if d.eng == op.eng and op.chan is None:
            if op.eng == "pe":
                return False
            return self.same_engine_sync
        return True

    def emit(self, stack):
        nc = self.nc
        for op in self.ops:
            for d in op.deps:
                if d.chan is None and self._needs_wait(op, d):
                    d.marked = True
        cnt = {}
        for op in self.ops:
            if op.chan is None and op.marked:
                cnt[op.eng] = cnt.get(op.eng, 0) + 1
                op.val = cnt[op.eng]
        engs = sorted({op.eng for op in self.ops})
        sems = {}
        for e in engs:
            sems[e] = stack.enter_context(nc.semaphore("s_" + e))
        for i, c in enumerate(self.chan_n):
            sems[("c", c)] = stack.enter_context(nc.semaphore("c_%d" % i))
        chan_final = {c: 16 * n for c, n in self.chan_n.items()}
        per = {e: [op for op in self.ops if op.eng == e] for e in engs}
        block = stack.enter_context(nc.Block())

        def event(d):
            if d.chan is not None:
                if self.chan_bulk.get(d.chan):
                    return ("c", d.chan), chan_final[d.chan]
                return ("c", d.chan), d.val
            return d.eng, d.val

        def run(e, engobj):
            waited = {}
            for op in per[e]:
                need = {}
                for d in op.deps:
                    if not self._needs_wait(op, d):
                        continue
                    k, v = event(d)
                    if need.get(k, 0) < v:
                        need[k] = v
                for k, v in need.items():
                    if waited.get(k, 0) < v:
                        engobj.wait_ge(sems[k], v)
                        waited[k] = v
                ins = op.fn(engobj)
                if op.chan is not None:
                    ins.then_inc(sems[("c", op.chan)], 16)
                elif op.marked:
                    ins.then_inc(sems[e], 1)
            for c in self.chan_n:
                if any(o.chan == c for o in per[e]):
                    engobj.wait_ge(sems[("c", c)], chan_final[c])

        deco = {"pe": block.tensor, "act": block.scalar, "dve": block.vector,
                "pool": block.gpsimd, "sp": block.sync}
        for e in engs:
            def mk(e):
                def f(engobj):
                    run(e, engobj)
                return f
            deco[e](mk(e))


import os
XQ = os.environ.get("XQ", "sp")
LN_LNEXP = os.environ.get("LN_LNEXP", "0") == "1"
LN_POW = os.environ.get("LN_POW", "0") == "1"
FAST_RECIP = os.environ.get("FAST_RECIP", "0") == "1"


def RECIP(e, out, in_):
    if FAST_RECIP:
        return e.reciprocal_approx_fast(out=out, in_=in_)
    return e.reciprocal(out=out, in_=in_)


def build_program(same_engine_sync=True, n_tiles=NT, stage=99):
    nc = bass.Bass("TRN2", target_bir_lowering=False)
    stack = ExitStack()
    P = Prog(nc, same_engine_sync=same_engine_sync)

    def din(name, shape):
        return nc.dram_tensor(name, list(shape), F32, kind="ExternalInput").ap()

    x_d = din("x", [S, D])
    wg_d = [din("ffn1_w_gate", [1, D, DFF]), din("ffn2_w_gate", [1, D, DFF])]
    wu_d = [din("ffn1_w_up", [1, D, DFF]), din("ffn2_w_up", [1, D, DFF])]
    wd_d = [din("ffn1_w_down", [1, DFF, D]), din("ffn2_w_down", [1, DFF, D])]
    lng_d = [din("ln1_g", [1, D]), din("ln2_g", [1, D]), din("ln3_g", [1, D])]
    lnb_d = [din("ln1_b", [1, D]), din("ln2_b", [1, D]), din("ln3_b", [1, D])]
    win_d = din("w_in", [1, D, IN_COLS])
    bf_d = din("b_forget", [1, 8])
    smallp_d = din("smallp", [128, 32])
    wa_d = din("rg_wa", [1, 8, 64, 64])
    wx_d = din("rg_wx", [1, 8, 64, 64])
    wout_d = din("w_out", [1, D, D])
    out_d = nc.dram_tensor("out", [S, D], F32, kind="ExternalOutput").ap()
    scr = nc.dram_tensor("wscr", [NSTREAM, 128, 2048], BF16, kind="Internal").ap()

    def sb(name, shape, dt=F32):
        return stack.enter_context(nc.sbuf_tensor(name, list(shape), dt))

    kT = sb("kT", [128, 4, S], BF16)
    Vb = sb("Vb", [128, NCH, 768], BF16)
    ws = sb("ws", [128, NSLOT, 2048], BF16)
    xres = sb("xres", [128, 4, D])
    xT = sb("xT", [128, 8, T], BF16)
    hT = sb("hT", [128, 32, T], BF16)
    hT_flat = hT[:].rearrange("p a b -> p (a b)")
    qT = hT[:, 0:4, :]
    lx_all = hT_flat[:, 4 * 512:16 * 512].bitcast(F32).rearrange("p (c n) -> p c n", c=4)
    gel_all = hT_flat[:, 16 * 512:24 * 512].bitcast(F32).rearrange("p (c n) -> p c n", c=4)
    mixT = hT[:, 24:32, :]
    stg = hT_flat.bitcast(F32).rearrange("p (s n) -> p s n", s=4)

    def k_qT(c): return [("hT", c)]
    def k_lx(c): return [("hT", 4 + 3 * c + i) for i in range(3)]
    def k_gel(c): return [("hT", 16 + 2 * c), ("hT", 17 + 2 * c)]
    def k_mix(mc): return [("hT", 24 + mc)]
    def k_stg(s): return [("hT", 8 * s + i) for i in range(8)]
    K_XT = [("xT", tc) for tc in range(4)]

    lru_u = sb("lru_u", [128, T])
    lru_r = sb("lru_r", [128, T])
    lru_s = sb("lru_s", [128, T])
    lru_g = sb("lru_g", [128, T])
    lru_h = sb("lru_h", [128, T])
    Pt = sb("Pt", [128, 6, T], BF16)
    rinv = sb("rinv", [128, 1, T])
    lnp = sb("lnp", [128, 2, 2, D])
    sg = sb("sg", [128, 2, T])
    ident = sb("ident", [128, 128])
    tri = sb("tri", [128, 128])
    e127 = sb("e127", [128, 128])
    wabd = sb("wabd", [128, 4, 128], BF16)
    wxbd = sb("wxbd", [128, 4, 128], BF16)
    lru_u16 = sb("lru_u16", [128, T], BF16)
    wfg32 = sb("wfg32", [128, 8, 8])
    wfg = sb("wfg", [128, 8, 8], BF16)
    cumT = sb("cumT", [128, NCH, 8])
    biasT = sb("biasT", [128, NCH, 8])
    refbc = sb("refbc", [128, 8])
    Vs = sb("Vs", [128, 6, 128], BF16)
    smallp = sb("smallp_sb", [128, 32])
    lamc = sb("lamc", [128, 16])
    bfb = sb("bfb", [128, 8])
    fgt = sb("fgt", [128, 4, 2, 8])
    state = sb("state", [128, 4])
    halo = sb("halo", [128, 4, 3])
    stat6 = sb("stat6", [128, 4, 12])
    mv = sb("mv", [128, 4, 2])
    lnsm = sb("lnsm", [128, 4, 4])
    neghalf = sb("neghalf", [128, 1])

    pbig = stack.enter_context(nc.psum_tensor("pbig", [128, 8, 512], F32))
    banks = [pbig[:, b, :] for b in range(8)]

    def kb(b): return ("ps", b)

    def mm(out, lhsT, rhs, start, stop, reads, writes):
        P.add("pe", lambda e: e.matmul(out, lhsT=lhsT, rhs=rhs, start=start, stop=stop), reads=reads, writes=writes)

    def tr(out, in_, reads, writes):
        P.add("pe", lambda e: e.transpose(out=out, in_=in_, identity=ident[:]), reads=list(reads) + ["ident"], writes=writes)

    def act(out, in_, func, reads, writes, **kw):
        P.add("act", lambda e: e.activation(out=out, in_=in_, func=func, **kw), reads=reads, writes=writes)

    def dma(eng, out, in_, reads, writes, chan, bulk=False, **kw):
        P.add(eng, lambda e: e.dma_start(out=out, in_=in_, **kw), reads=reads, writes=writes, chan=chan, bulk=bulk)

    def vcopy(eng, out, in_, reads, writes):
        P.add(eng, lambda e: e.tensor_copy(out=out, in_=in_), reads=reads, writes=writes)

    def tt(eng, out, in0, in1, op, reads, writes):
        P.add(eng, lambda e: e.tensor_tensor(out=out, in0=in0, in1=in1, op=op), reads=reads, writes=writes)

    def ts(eng, out, in0, s1, s2, op0, op1, reads, writes):
        P.add(eng, lambda e: e.tensor_scalar(out=out, in0=in0, scalar1=s1, scalar2=s2, op0=op0, op1=op1), reads=reads, writes=writes)

    def stt(out, in0, scalar, in1, op0, op1, reads, writes):
        P.add("dve", lambda e: e.scalar_tensor_tensor(out=out, in0=in0, scalar=scalar, in1=in1, op0=op0, op1=op1), reads=reads, writes=writes)

    def memset(eng, ap, val, writes):
        P.add(eng, lambda e: e.memset(ap, val), writes=writes)

    for tc_ in range(4):
        dma(XQ, xres[:, tc_, :], x_d[tc_ * 128:(tc_ + 1) * 128, :], [], [("xres", tc_)], ("xld", tc_))

    memset("pool", ident[:], 1.0, ["ident"])
    P.add("pool", lambda e: e.affine_select(out=ident[:], in_=ident[:], pattern=[[1, 128]], compare_op=ALU.is_equal,
                                            fill=0.0, base=0, channel_multiplier=-1), reads=["ident"], writes=["ident"])
    memset("pool", tri[:], 1.0, ["tri"])
    P.add("pool", lambda e: e.affine_select(out=tri[:], in_=tri[:], pattern=[[1, 128]], compare_op=ALU.is_ge,
                                            fill=0.0, base=0, channel_multiplier=-1), reads=["tri"], writes=["tri"])
    memset("pool", e127[:], 1.0, ["e127"])
    P.add("pool", lambda e: e.affine_select(out=e127[:], in_=e127[:], pattern=[[0, 128]], compare_op=ALU.is_equal,
                                            fill=0.0, base=-127, channel_multiplier=1), reads=["e127"], writes=["e127"])
    wtmp = Pt[:, 0:4, :].rearrange("p a b -> p (a b)").bitcast(F32).rearrange("p (w c n) -> p w c n", w=2, c=4)
    memset("pool", wtmp, 0.0, [("P", k_) for k_ in range(4)])
    memset("dve", state[:], 0.0, ["state"])
    memset("dve", neghalf[:], -0.5, ["neghalf"])
    memset("dve", halo[:], 0.0, ["halo"])

    for c in range(4):
        for hf in range(2):
            dma("sp", wtmp[hf * 64:(hf + 1) * 64, 0, c, hf * 64:(hf + 1) * 64], wa_d[0, 2 * c + hf], [], [("P", k_) for k_ in range(4)], "par", bulk=True)
            dma("sp", wtmp[hf * 64:(hf + 1) * 64, 1, c, hf * 64:(hf + 1) * 64], wx_d[0, 2 * c + hf], [], [("P", k_) for k_ in range(4)], "par", bulk=True)
    dma("sp", smallp[:], smallp_d, [], ["smallp"], "par", bulk=True)
    dma("sp", bfb[:], bf_d[0].partition_broadcast(128), [], ["bfb"], "par", bulk=True)
    dma("sp", wfg32[:], win_d[0][:, 2560:2568].rearrange("(dc p) c -> p dc c", p=128), [], ["wfg32"], "par", bulk=True)
    vcopy("dve", wfg[:], wfg32[:], ["wfg32"], ["wfg"])
    vcopy("dve", wabd[:], wtmp[:, 0, :, :], [("P", k_) for k_ in range(4)], ["wabd"])
    vcopy("dve", wxbd[:], wtmp[:, 1, :, :], [("P", k_) for k_ in range(4)], ["wxbd"])
    act(lamc[:, 0:4], smallp[:, 28:32], AF.Exp, ["smallp"], ["lamc"], scale=-1.0)
    act(lamc[:, 0:4], lamc[:, 0:4], AF.Ln, ["lamc"], ["lamc"], bias=1.0)
    ts("dve", lamc[:, 4:8], lamc[:, 0:4], -16.0, None, ALU.mult, ALU.bypass, ["lamc"], ["lamc"])
    ts("dve", lamc[:, 0:4], lamc[:, 0:4], -8.0, None, ALU.mult, ALU.bypass, ["lamc"], ["lamc"])
    ts("dve", lamc[:, 8:16], smallp[:, 20:28], -1.0, None, ALU.mult, ALU.bypass, ["smallp"], ["lamc"])

    def src_cols(w2d, c0, ncols):
        return w2d[:, c0:c0 + ncols].rearrange("(dc p) c -> p dc c", p=128)

    def src_wd(w2d, fb, dh):
        return w2d[fb * 512:(fb + 1) * 512, dh * 512:(dh + 1) * 512].rearrange("(f p) c -> p f c", p=128)

    catalog = []
    for k in range(2):
        ent = []
        for fb2 in range(16):
            ent.append(src_cols(wg_d[k][0], fb2 * 256, 256))
            ent.append(src_cols(wu_d[k][0], fb2 * 256, 256))
        for dh in range(2):
            for fb in range(8):
                ent.append(src_wd(wd_d[k][0], fb, dh))
        if k == 0:
            catalog += ent
            for cb in range(10):
                catalog.append(src_cols(win_d[0], cb * 256, 256))
            for blk in range(4):
                catalog.append(src_cols(wout_d[0], blk * 256, 256))
        else:
            catalog += ent
    assert len(catalog) == NSTREAM

    NSTG = 3
    LA = 3
    cast_engs = ["dve", "act", "pool"]
    TOTAL = n_tiles * NSTREAM
    conv = [0] * NSTREAM

    NSTG4 = 4

    def stg_ap(s_):
        return Vb[:, 4 + 6 * s_:10 + 6 * s_, :].rearrange("p a b -> p (a b)")[:, 0:4096].bitcast(F32)

    def k_stg(s_):
        return [("V", 4 + 6 * s_ + q) for q in range(6)]

    pumped = [0]
    loaded = [0]
    pending_st = []

    def flush_stores(upto):
        while pending_st and pending_st[0][0] <= upto:
            _, e2, sl2 = pending_st.pop(0)
            dma("sp", scr[e2], ws[:, sl2, :], [("ws", sl2)], [("scr", e2)], ("scrst", sl2))

    def load_fp32(m):
        e = m % NSTREAM
        s_ = m % NSTG4
        src = catalog[e]
        a_ = src.shape[1]
        dma("sp", stg_ap(s_).rearrange("p (a b) -> p a b", a=a_), src, [], k_stg(s_), ("stg", s_))

    def pump(m):
        tile_m, e = divmod(m, NSTREAM)
        sl = m % NSLOT
        if tile_m > conv[e]:
            dma("sp", ws[:, sl, :], scr[e], [("scr", e)], [("ws", sl)], ("ws", sl))
        else:
            while loaded[0] <= min(m + 1, NSTREAM - 1):
                load_fp32(loaded[0])
                loaded[0] += 1
            s_ = m % NSTG4
            ce = cast_engs[m % len(cast_engs)]
            copy_any(ce, ws[:, sl, :], stg_ap(s_), k_stg(s_), [("ws", sl)])
            if tile_m == conv[e]:
                pending_st.append((m, e, sl))
        flush_stores(m - 3)

    stream_pos = [0]

    def next_slot(tile_n):
        n = stream_pos[0]
        stream_pos[0] += 1
        while pumped[0] <= min(n + LA, TOTAL - 1):
            pump(pumped[0])
            pumped[0] += 1
        if n == TOTAL - 1:
            flush_stores(TOTAL)
        return n % NSLOT

    ln_phase = [0]

    def ln_params(which):
        n = ln_phase[0]
        ln_phase[0] += 1
        r = n % 2
        dma("sp", lnp[:, r, 0, :], lng_d[which][0].partition_broadcast(128), [], [("lnp", r)], ("lnp", r))
        dma("sp", lnp[:, r, 1, :], lnb_d[which][0].partition_broadcast(128), [], [("lnp", r)], ("lnp", r))
        return r

    rr = {"evac": 0}

    def evac_engine():
        rr["evac"] += 1
        return "act" if rr["evac"] % 2 else "dve"

    def copy_any(eng, out, in_, reads, writes):
        if eng == "act":
            P.add("act", lambda e: e.copy(out=out, in_=in_), reads=reads, writes=writes)
        else:
            vcopy(eng, out, in_, reads, writes)

    def make_xT():
        for tc in range(4):
            for g in range(2):
                b = (tc * 2 + g) % 8
                for j in range(4):
                    dc = 4 * g + j
                    tr(banks[b][:, j * 128:(j + 1) * 128], xres[:, tc, dc * 128:(dc + 1) * 128], [("xres", tc)], [kb(b)])
                copy_any(evac_engine(), xT[:, 4 * g:4 * g + 4, tc * 128:(tc + 1) * 128],
                         banks[b][:].rearrange("p (j n) -> p j n", j=4), [kb(b)], [("xT", tc)])

    sg_flat = sg[:].rearrange("p a b -> p (a b)")
    qzB = sg_flat.bitcast(BF16).rearrange("p (h n) -> p h n", h=4)

    def qz(h):
        return qT[:, h, :] if h < 4 else qzB[:, h - 4, :]

    def k_qz(h):
        return [("hT", h)] if h < 4 else [("sg", (h - 4) // 2)]

    def prefetch_xT(tile_n, tc):
        r0 = tile_n * T + tc * 128
        ksg = [("sg", 0), ("sg", 1)]
        dma("sp", sg_flat, x_d[r0:r0 + 128, :], [], ksg, "xpf")
        for g in range(2):
            b = (tc % 2) * 2 + g
            for j in range(4):
                dc = 4 * g + j
                tr(banks[b][:, j * 128:(j + 1) * 128], sg_flat[:, dc * 128:(dc + 1) * 128], ksg, [kb(b)])
            copy_any(evac_engine(), xT[:, 4 * g:4 * g + 4, tc * 128:(tc + 1) * 128],
                     banks[b][:].rearrange("p (j n) -> p j n", j=4), [kb(b)], [("xT", tc)])

    def ln_stats(tc, hh):
        P.add("dve", lambda e: e.bn_stats(out=stat6[:, tc, hh * 6:(hh + 1) * 6], in_=xres[:, tc, hh * 512:(hh + 1) * 512]),
              reads=[("xres", tc)], writes=[("stat6", tc, hh)])

    def layer_norm(tc, which_r, final_tile=None):
        kx = ("xres", tc)
        P.add("dve", lambda e: e.bn_aggr(out=mv[:, tc, :], in_=stat6[:, tc, :]), reads=[("stat6", tc, 0), ("stat6", tc, 1)], writes=[("mv", tc)])
        if LN_POW:
            ts("pool", lnsm[:, tc, 0:1], mv[:, tc, 1:2], EPS_P, 1.0, ALU.add, ALU.mult, [("mv", tc)], [("lnsm", tc)])
            tt("pool", lnsm[:, tc, 1:2], lnsm[:, tc, 0:1], neghalf[:, 0:1], ALU.pow, [("lnsm", tc), "neghalf"], [("lnsm", tc)])
        elif LN_LNEXP:
            act(lnsm[:, tc, 0:1], mv[:, tc, 1:2], AF.Ln, [("mv", tc)], [("lnsm", tc)], bias=EPS_P, scale=1.0)
            act(lnsm[:, tc, 1:2], lnsm[:, tc, 0:1], AF.Exp, [("lnsm", tc)], [("lnsm", tc)], scale=-0.5)
        else:
            act(lnsm[:, tc, 0:1], mv[:, tc, 1:2], AF.Sqrt, [("mv", tc)], [("lnsm", tc)], bias=EPS_P, scale=1.0)
            P.add("dve", lambda e: e.reciprocal(out=lnsm[:, tc, 1:2], in_=lnsm[:, tc, 0:1]), reads=[("lnsm", tc)], writes=[("lnsm", tc)])
        ts("dve", lnsm[:, tc, 2:3], mv[:, tc, 0:1], -1.0, lnsm[:, tc, 1:2], ALU.mult, ALU.mult, [("mv", tc), ("lnsm", tc)], [("lnsm", tc)])
        act(xres[:, tc, :], xres[:, tc, :], AF.Identity, [kx, ("lnsm", tc)], [kx], scale=lnsm[:, tc, 1:2], bias=lnsm[:, tc, 2:3])
        tt("dve", xres[:, tc, :], xres[:, tc, :], lnp[:, which_r, 0, :], ALU.mult, [kx, ("lnp", which_r)], [kx])
        tt("pool", xres[:, tc, :], xres[:, tc, :], lnp[:, which_r, 1, :], ALU.add, [kx, ("lnp", which_r)], [kx])

    def ln_tail(which_r, M, evac, do_xT):
        def A(tc):
            evac(tc)
            P.add("dve", lambda e: e.bn_aggr(out=mv[:, tc, :], in_=stat6[:, tc, :]), reads=[("stat6", tc, 0), ("stat6", tc, 1)], writes=[("mv", tc)])

        def B(tc):
            act(lnsm[:, tc, 0:1], mv[:, tc, 1:2], AF.Sqrt, [("mv", tc)], [("lnsm", tc)], bias=EPS_P, scale=1.0)

        def C(tc):
            P.add("dve", lambda e: e.reciprocal(out=lnsm[:, tc, 1:2], in_=lnsm[:, tc, 0:1]), reads=[("lnsm", tc)], writes=[("lnsm", tc)])
            ts("dve", lnsm[:, tc, 2:3], mv[:, tc, 0:1], -1.0, lnsm[:, tc, 1:2], ALU.mult, ALU.mult, [("mv", tc), ("lnsm", tc)], [("lnsm", tc)])

        def D(tc):
            act(xres[:, tc, :], xres[:, tc, :], AF.Identity, [("xres", tc), ("lnsm", tc)], [("xres", tc)], scale=lnsm[:, tc, 1:2], bias=lnsm[:, tc, 2:3])

        def E(tc):
            tt("dve", xres[:, tc, :], xres[:, tc, :], lnp[:, which_r, 0, :], ALU.mult, [("xres", tc), ("lnp", which_r)], [("xres", tc)])

        def F(tc):
            tt("pool", xres[:, tc, :], xres[:, tc, :], lnp[:, which_r, 1, :], ALU.add, [("xres", tc), ("lnp", which_r)], [("xres", tc)])

        def G(tc):
            if not do_xT:
                return
            for g in range(2):
                b = 4 + 2 * (tc % 2) + g
                for j in range(4):
                    dc = 4 * g + j
                    tr(banks[b][:, j * 128:(j + 1) * 128], xres[:, tc, dc * 128:(dc + 1) * 128], [("xres", tc)], [kb(b)])
                copy_any("act", xT[:, 4 * g:4 * g + 4, tc * 128:(tc + 1) * 128],
                         banks[b][:].rearrange("p (j n) -> p j n", j=4), [kb(b)], [("xT", tc)])

        def Mf(tc):
            if M is not None:
                M(tc)

        st = {"M": Mf, "A": A, "B": B, "C": C, "D": D, "E": E, "F": F, "G": G}
        order = "M0 A0 B0 M1 A1 B1 C0 D0 M2 A2 B2 C1 D1 E0 F0 M3 A3 B3 C2 D2 E1 F1 G0 C3 D3 E2 F2 G1 E3 F3 G2 G3"
        for tok in order.split():
            st[tok[0]](int(tok[1]))

    def ffn(tile_n, which_ln, c1, prefetch_tile=None, hook=None):
        for fb2 in range(16):
            if hook is not None and fb2 == 4:
                hook()
            sg_ = next_slot(tile_n)
            su_ = next_slot(tile_n)
            for fcl in range(2):
                fc = 2 * fb2 + fcl
                par = fc % 2
                bg, bu = 2 * par, 2 * par + 1
                for dc in range(8):
                    mm(banks[bg][:], ws[:, sg_, dc * 256 + fcl * 128: dc * 256 + (fcl + 1) * 128], xT[:, dc, :], dc == 0, dc == 7,
                       [("ws", sg_)] + K_XT, [kb(bg)])
                for dc in range(8):
                    mm(banks[bu][:], ws[:, su_, dc * 256 + fcl * 128: dc * 256 + (fcl + 1) * 128], xT[:, dc, :], dc == 0, dc == 7,
                       [("ws", su_)] + K_XT, [kb(bu)])
                act(sg[:, par, :], banks[bg][:], AF.Silu, [kb(bg)], [("sg", par)])
                tt("dve", hT[:, fc, :], sg[:, par, :], banks[bu][:], ALU.mult, [("sg", par), kb(bu)], [("hT", fc)])
        r = ln_params(which_ln)
        for dh in range(2):
            bb = [4, 5, 6, 7] if dh == 0 else [0, 1, 2, 3]
            nfb = 8 if dh == 0 else 6
            for fb in range(nfb):
                sl = next_slot(tile_n)
                for fcl in range(4):
                    fc = 4 * fb + fcl
                    for tc in range(4):
                        mm(banks[bb[tc]][:], hT[:, fc, tc * 128:(tc + 1) * 128], ws[:, sl, fcl * 512:(fcl + 1) * 512], fc == 0, fc == 31,
                           [("hT", fc), ("ws", sl)], [kb(bb[tc])])
                if prefetch_tile is not None and dh == 0 and fb % 2 == 0:
                    prefetch_xT(prefetch_tile, fb // 2)
            if dh == 0:
                for tc in range(4):
                    stt(xres[:, tc, 0:512], banks[bb[tc]][:], c1, xres[:, tc, 0:512], ALU.mult, ALU.add,
                        [kb(bb[tc]), ("xres", tc)], [("xres", tc)])
                    ln_stats(tc, 0)
            else:
                tail_slots = [next_slot(tile_n), next_slot(tile_n)]

                def M(tc, bb=bb, tail_slots=tail_slots):
                    for q_, sl_ in enumerate(tail_slots):
                        for fcl in range(4):
                            fc = 4 * (6 + q_) + fcl
                            mm(banks[bb[tc]][:], hT[:, fc, tc * 128:(tc + 1) * 128], ws[:, sl_, fcl * 512:(fcl + 1) * 512], False, fc == 31,
                               [("hT", fc), ("ws", sl_)], [kb(bb[tc])])

                def evac(tc, bb=bb):
                    stt(xres[:, tc, 512:1024], banks[bb[tc]][:], c1, xres[:, tc, 512:1024], ALU.mult, ALU.add,
                        [kb(bb[tc]), ("xres", tc)], [("xres", tc)])
                    ln_stats(tc, 1)

                ln_tail(r, M, evac, do_xT=(which_ln == 0))

    def mixer(i):
        bank_rr = [0]

        def nb():
            b = bank_rr[0] % 8
            bank_rr[0] += 1
            return b

        P.add("pool", lambda e: e.memset(qT[:, 0:4, :], 0.0), writes=[("hT", c_) for c_ in range(4)])
        P.add("pool", lambda e: e.memset(qzB[:], 0.0), writes=[("sg", 0), ("sg", 1)])
        ones_dst = Vb[:, 4 * i:4 * i + 4, :].rearrange("p j (q t) -> p j q t", q=4)[:, :, :, 64:128]
        P.add("pool", lambda e: e.memset(ones_dst, 1.0), writes=[("V", 4 * i + tc) for tc in range(4)])

        def proj_chunk(sl, half, cc):
            b = nb()
            for dc in range(8):
                mm(banks[b][:], ws[:, sl, dc * 256 + half * 128: dc * 256 + (half + 1) * 128], xT[:, dc, :], dc == 0, dc == 7,
                   [("ws", sl)] + K_XT, [kb(b)])
            if cc < 4:
                for e2 in range(2):
                    h_ = 2 * cc + e2
                    P.add("act", (lambda h_=h_, e2=e2, b=b: lambda e: e.activation(out=qz(h_)[e2 * 64:(e2 + 1) * 64, :], in_=banks[b][e2 * 64:(e2 + 1) * 64, :],
                                                                             func=AF.Copy, scale=0.125))(),
                          reads=[kb(b)], writes=k_qz(h_))
            elif cc < 8:
                vcopy("dve", kT[:, cc - 4, i * T:(i + 1) * T], banks[b][:], [kb(b)], [("kT", cc - 4, i)])
            elif cc < 16:
                c = cc - 12
                copy_any(evac_engine(), lx_all[:, c, 3:3 + T], banks[b][:], [kb(b)], k_lx(c))
            else:
                c = cc - 16
                act(gel_all[:, c, :], banks[b][:], AF.Gelu_apprx_tanh, [kb(b)], k_gel(c))

        for cb in range(4):
            sl = next_slot(i)
            for half in range(2):
                proj_chunk(sl, half, 2 * cb + half)
        v_slots = [next_slot(i), next_slot(i)]
        for tc in range(4):
            j = 4 * i + tc
            b = nb()
            for vi, vs in enumerate(v_slots):
                for dc in range(8):
                    mm(banks[b][:, vi * 256:(vi + 1) * 256], xT[:, dc, tc * 128:(tc + 1) * 128], ws[:, vs, dc * 256:(dc + 1) * 256],
                       dc == 0, dc == 7, [("ws", vs), ("xT", tc)], [kb(b)])
            src = banks[b][:].rearrange("p (q e d) -> p q e d", q=4, e=2)
            dst = Vb[:, j, :].rearrange("p (q t) -> p q t", q=4)
            vcopy("dve", dst[:, :, 0:64], src[:, :, 0, :], [kb(b)], [("V", j)])
            P.add("act", (lambda dst=dst, src=src: lambda e: e.copy(out=dst[:, :, 128:192], in_=src[:, :, 1, :]))(), reads=[kb(b)], writes=[("V", j)])
            b2 = nb()
            for dc in range(8):
                mm(banks[b2][:, 0:8], xT[:, dc, tc * 128:(tc + 1) * 128], wfg[:, dc, :], dc == 0, dc == 7, [("xT", tc), "wfg"], [kb(b2)])
            tt("dve", fgt[:, tc, 0, :], banks[b2][:, 0:8], bfb[:], ALU.add, [kb(b2), "bfb"], [("fgt0", tc)])
            act(fgt[:, tc, 1, :], fgt[:, tc, 0, :], AF.Exp, [("fgt0", tc)], [("fgt1", tc)], scale=-1.0)
            act(fgt[:, tc, 1, :], fgt[:, tc, 1, :], AF.Ln, [("fgt1", tc)], [("fgt1", tc)], bias=1.0)
        def cumsum_step(tc):
            j = 4 * i + tc
            b3 = nb()
            mm(banks[b3][:, 0:8], tri[:], fgt[:, tc, 1, :], True, j == 0, ["tri", ("fgt1", tc)], [kb(b3)])
            if j > 0:
                mm(banks[b3][:, 0:8], e127[:], cumT[:, j - 1, :], False, True, ["e127", ("cumT", j - 1)], [kb(b3)])
            vcopy("dve", cumT[:, j, :], banks[b3][:, 0:8], [kb(b3)], [("cumT", j)])

        for cb in range(6, 10):
            sl = next_slot(i)
            for half in range(2):
                proj_chunk(sl, half, 2 * cb + half)
            cumsum_step(cb - 6)

        nj = 4 * i + 4
        bref = nb()
        mm(banks[bref][:, 0:8], e127[:], cumT[:, 4 * i + 1, :], True, True, ["e127", ("cumT", 4 * i + 1)], [kb(bref)])
        vcopy("dve", refbc[:], banks[bref][:, 0:8], [kb(bref)], ["refbc"])
        tt("dve", biasT[:, 0:nj, :], cumT[:, 0:nj, :], refbc[:].unsqueeze(1).to_broadcast([128, nj, 8]), ALU.subtract,
           [("cumT", j) for j in range(nj)] + ["refbc"], ["biasT"])

        s_rr = [0]

        def lru_conv(c):
            vcopy("pool", lx_all[:, c, 0:3], halo[:, c, :], ["halo"], k_lx(c))
            ts("dve", lru_u[:], lx_all[:, c, 0:T], smallp[:, c * 4:c * 4 + 1], smallp[:, 16 + c:17 + c], ALU.mult, ALU.add,
               k_lx(c) + ["smallp"], ["lru_u"])
            for k in range(1, 4):
                stt(lru_u[:], lx_all[:, c, k:k + T], smallp[:, c * 4 + k:c * 4 + k + 1], lru_u[:], ALU.mult, ALU.add,
                    k_lx(c) + ["smallp", "lru_u"], ["lru_u"])
            vcopy("pool", halo[:, c, :], lx_all[:, c, T:T + 3], k_lx(c), ["halo"])
            vcopy("dve", lru_u16[:], lru_u[:], ["lru_u"], ["lru_u16"])

        group_taker = [None]

        def lru_gates(c):
            g_ = group_taker[0]()
            ba_ = 2 * g_
            bx_ = 2 * g_ + 1
            mm(banks[ba_][:], wabd[:, c, :], lru_u16[:], True, True, ["wabd", "lru_u16"], [kb(ba_)])
            mm(banks[bx_][:], wxbd[:, c, :], lru_u16[:], True, True, ["wxbd", "lru_u16"], [kb(bx_)])
            act(lru_r[:], banks[ba_][:], AF.Exp, [kb(ba_), "lamc"], ["lru_r"], scale=-1.0, bias=lamc[:, 8 + c:9 + c])
            act(lru_g[:], banks[bx_][:], AF.Exp, [kb(bx_), "lamc"], ["lru_g"], scale=-1.0, bias=lamc[:, 12 + c:13 + c])
            ts("pool", lru_r[:], lru_r[:], 1.0, 1.0, ALU.add, ALU.mult, ["lru_r"], ["lru_r"])
            P.add("dve", lambda e: RECIP(e, lru_r[:], lru_r[:]), reads=["lru_r"], writes=["lru_r"])
            ts("pool", lru_g[:], lru_g[:], 1.0, 1.0, ALU.add, ALU.mult, ["lru_g"], ["lru_g"])
            P.add("dve", lambda e: RECIP(e, lru_g[:], lru_g[:]), reads=["lru_g"], writes=["lru_g"])
            tt("dve", lru_g[:], lru_g[:], lru_u[:], ALU.mult, ["lru_g", "lru_u"], ["lru_g"])

        def lru_rest(c):
            act(lru_s[:], lru_r[:], AF.Exp, ["lru_r", "lamc"], ["lru_s"], scale=lamc[:, 4 + c:5 + c])
            act(lru_r[:], lru_r[:], AF.Exp, ["lru_r", "lamc"], ["lru_r"], scale=lamc[:, c:c + 1])
            act(lru_s[:], lru_s[:], AF.Ln, ["lru_s"], ["lru_s"], scale=-1.0, bias=1.0)
            act(lru_s[:], lru_s[:], AF.Exp, ["lru_s"], ["lru_s"], scale=0.5)
            tt("pool", lru_s[:], lru_s[:], lru_g[:], ALU.mult, ["lru_s", "lru_g"], ["lru_s"])
            P.add("dve", (lambda c=c: lambda e: e.tensor_tensor_scan(out=lru_h[:], data0=lru_r[:], data1=lru_s[:], initial=state[:, c:c + 1],
                                                                      op0=ALU.mult, op1=ALU.add))(),
                  reads=["lru_r", "lru_s", "state"], writes=["lru_h"])
            vcopy("dve", state[:, c:c + 1], lru_h[:, T - 1:T], ["lru_h"], ["state"])
            tt("dve", mixT[:, 4 + c, :], gel_all[:, c, :], lru_h[:], ALU.mult, k_gel(c) + ["lru_h"], k_mix(4 + c))

        act(biasT[:, 0:nj, :], biasT[:, 0:nj, :], AF.Exp, ["biasT"], ["biasT"])
        NG = 3
        held = set()
        last_g = [NG - 1]

        def take_group():
            for d_ in range(1, NG + 1):
                g = (last_g[0] + d_) % NG
                if g not in held:
                    last_g[0] = g
                    return g
            raise AssertionError("no free score group")

        group_taker[0] = take_group

        def cols_of(j):
            jj = j - 4 * i
            return (jj * 128 if jj > 0 else 0), T

        def issue_S(u):
            h, kind, j, first, last = u
            kc = h // 2
            vcol = kc * 192 + (h % 2) * 64
            if first and h % 2 == 0:
                lru_conv(h // 2)
            g = take_group()
            held.add(g)
            if kind == "pair":
                for q_ in range(2):
                    bk_ = 2 * g + q_
                    mm(banks[bk_][:, :], kT[:, kc, (j + q_) * 128:(j + q_ + 1) * 128], qz(h)[:, :], True, True,
                       [("kT", kc, (j + q_) // 4)] + k_qz(h), [kb(bk_)])
                act(Pt[:, 2 * g:2 * g + 2, :], pbig[:, 2 * g:2 * g + 2, :], AF.Exp, [kb(2 * g), kb(2 * g + 1)], [("P", 2 * g), ("P", 2 * g + 1)])
            else:
                bk_ = 2 * g
                c0, c1_ = cols_of(j)
                mm(banks[bk_][:, c0:c1_], kT[:, kc, j * 128:(j + 1) * 128], qz(h)[:, c0:c1_], True, True,
                   [("kT", kc, j // 4)] + k_qz(h), [kb(bk_)])
                act(Pt[:, bk_, c0:c1_], banks[bk_][:, c0:c1_], AF.Exp, [kb(bk_)], [("P", bk_)])
                P.add("pool", (lambda k=bk_, c0=c0: lambda e: e.affine_select(out=Pt[:, k, c0:c0 + 128], in_=Pt[:, k, c0:c0 + 128], pattern=[[1, 128]],
                                                                               compare_op=ALU.is_ge, fill=0.0, base=0, channel_multiplier=-1))(),
                      reads=[("P", bk_)], writes=[("P", bk_)])
            for q_, jx in enumerate([j, j + 1] if kind == "pair" else [j]):
                r_ = 2 * g + q_
                ts("pool", Vs[:, r_, :], Vb[:, jx, vcol:vcol + 128], biasT[:, jx, h:h + 1], 1.0, ALU.mult, ALU.mult,
                   [("V", jx), "biasT"], [("Vs", r_)])
            return g

        def issue_PV(u, g):
            h, kind, j, first, last = u
            kc = h // 2
            ob = 6 + (h % 2)
            js = [j, j + 1] if kind == "pair" else [j]
            for q_, jx in enumerate(js):
                bk_ = 2 * g + q_
                c0, c1_ = cols_of(jx) if kind == "single" else (0, T)
                mm(banks[ob][:, c0:c1_], Vs[:, bk_, :], Pt[:, bk_, c0:c1_], first and q_ == 0, last and q_ == len(js) - 1,
                   [("Vs", bk_), ("P", bk_)], [kb(ob)])
            held.discard(g)
            if last:
                if h % 2 == 0:
                    P.add("dve", (lambda ob=ob: lambda e: RECIP(e, rinv[64:128, 0, :], banks[ob][64:128, :]))(), reads=[kb(ob)], writes=[("rinv", 0)])
                    tt("dve", mixT[0:64, kc, :], banks[ob][0:64, :], rinv[64:128, 0, :], ALU.mult, [kb(ob), ("rinv", 0)], k_mix(kc))
                    lru_gates(h // 2)
                else:
                    P.add("dve", (lambda ob=ob: lambda e: RECIP(e, rinv[0:64, 0, :], banks[ob][0:64, :]))(), reads=[kb(ob)], writes=[("rinv", 0)])
                    tt("dve", mixT[64:128, kc, :], banks[ob][64:128, :], rinv[0:64, 0, :], ALU.mult, [kb(ob), ("rinv", 0)], k_mix(kc))
                    lru_rest(h // 2)

        all_units = []
        for h in range(8):
            uh = [("pair", j) for j in range(0, 4 * i, 2)] + [("single", j) for j in range(4 * i, nj)]
            for q_, (kind, j) in enumerate(uh):
                all_units.append((h, kind, j, q_ == 0, q_ == len(uh) - 1))
        pend = []
        nxt = 0
        while nxt < min(2, len(all_units)):
            pend.append((all_units[nxt], issue_S(all_units[nxt])))
            nxt += 1
        while pend:
            u, g = pend.pop(0)
            if nxt < len(all_units):
                pend.append((all_units[nxt], issue_S(all_units[nxt])))
                nxt += 1
            issue_PV(u, g)

        r = ln_params(1)
        bb = [0, 1, 2, 3]
        for dh in range(2):
            for bk in range(2):
                sl = next_slot(i)
                if dh == 1 and bk == 1:
                    def M(tc, sl=sl):
                        for mc in range(8):
                            mm(banks[bb[tc]][:, 256:512], mixT[:, mc, tc * 128:(tc + 1) * 128], ws[:, sl, mc * 256:(mc + 1) * 256],
                               mc == 0, mc == 7, k_mix(mc) + [("ws", sl)], [kb(bb[tc])])

                    def evac(tc):
                        stt(xres[:, tc, 512:1024], banks[bb[tc]][:], 1.0 / ALPHA, xres[:, tc, 512:1024], ALU.mult, ALU.add,
                            [kb(bb[tc]), ("xres", tc)], [("xres", tc)])
                        ln_stats(tc, 1)

                    ln_tail(r, M, evac, do_xT=True)
                    continue
                for mc in range(8):
                    for tc in range(4):
                        mm(banks[bb[tc]][:, bk * 256:(bk + 1) * 256], mixT[:, mc, tc * 128:(tc + 1) * 128], ws[:, sl, mc * 256:(mc + 1) * 256],
                           mc == 0, mc == 7, k_mix(mc) + [("ws", sl)], [kb(bb[tc])])
            if dh == 0:
                for tc in range(4):
                    stt(xres[:, tc, 0:512], banks[bb[tc]][:], 1.0 / ALPHA, xres[:, tc, 0:512], ALU.mult, ALU.add,
                        [kb(bb[tc]), ("xres", tc)], [("xres", tc)])
                    ln_stats(tc, 0)

    def load_x(i):
        for tc in range(4):
            r0 = i * T + tc * 128
            dma(XQ, xres[:, tc, :], x_d[r0:r0 + 128, :], [], [("xres", tc)], ("xld", tc))

    def store_out(i):
        for tc in range(4):
            r0 = i * T + tc * 128
            dma(XQ, out_d[r0:r0 + 128, :], xres[:, tc, :], [("xres", tc)], [], ("ost", tc))

    pending = []
    for i in range(n_tiles):
        hook = None
        if pending:
            todo = list(pending)
            pending = []

            def hook(todo=todo):
                for f in todo:
                    f()
        if stage >= 1 and not (i > 0 and stage >= 4):
            make_xT()
        if stage >= 2:
            ffn(i, 0, 0.5 / ALPHA, hook=hook)
        elif hook is not None:
            hook()
        if stage >= 3:
            mixer(i)
        if stage >= 4:
            ffn(i, 2, 0.5 / ALPHA, prefetch_tile=(i + 1 if i + 1 < n_tiles else None))
        if i == n_tiles - 1:
            store_out(i)
        else:
            pending = [(lambda i=i: store_out(i)), (lambda i=i: load_x(i + 1))]

    P.emit(stack)
    stack.close()
    return nc


_CACHE = {}


def _pack_small(conv_w, conv_b, rg_ba, rg_bx, lru_lambda):
    sp = np.zeros((128, 32), np.float32)
    cw = np.asarray(conv_w, np.float32)[0]
    for c in range(4):
        for k in range(4):
            sp[:, c * 4 + k] = cw[k, c * 128:(c + 1) * 128]
    sp[:, 16:20] = np.asarray(conv_b, np.float32)[0].reshape(4, 128).T
    sp[:, 20:24] = np.asarray(rg_ba, np.float32)[0].reshape(4, 128).T
    sp[:, 24:28] = np.asarray(rg_bx, np.float32)[0].reshape(4, 128).T
    sp[:, 28:32] = np.asarray(lru_lambda, np.float32)[0].reshape(4, 128).T
    return sp


def kernel(x, ffn1_w_gate, ffn1_w_up, ffn1_w_down, ln1_g, ln1_b, w_in, b_forget,
           conv_w, conv_b, rg_wa, rg_ba, rg_wx, rg_bx, lru_lambda, w_out,
           ln2_g, ln2_b, ffn2_w_gate, ffn2_w_up, ffn2_w_down, ln3_g, ln3_b):
    if "nc" not in _CACHE:
        _CACHE["nc"] = build_program()
    nc = _CACHE["nc"]
    f = lambda a: np.ascontiguousarray(np.asarray(a, dtype=np.float32))
    shared = {
        "ffn1_w_gate": f(ffn1_w_gate), "ffn1_w_up": f(ffn1_w_up), "ffn1_w_down": f(ffn1_w_down),
        "ffn2_w_gate": f(ffn2_w_gate), "ffn2_w_up": f(ffn2_w_up), "ffn2_w_down": f(ffn2_w_down),
        "ln1_g": f(ln1_g), "ln1_b": f(ln1_b), "ln2_g": f(ln2_g), "ln2_b": f(ln2_b), "ln3_g": f(ln3_g), "ln3_b": f(ln3_b),
        "w_in": f(w_in), "b_forget": f(b_forget), "rg_wa": f(rg_wa), "rg_wx": f(rg_wx), "w_out": f(w_out),
        "smallp": _pack_small(conv_w, conv_b, rg_ba, rg_bx, lru_lambda),
    }
    xs = f(x)
    in_maps = []
    for b in range(8):
        m = dict(shared)
        m["x"] = xs[b]
        in_maps.append(m)
    res = run_bass_kernel_spmd(nc, in_maps, core_ids=list(range(8)))
    return np.stack([np.asarray(r["out"], dtype=np.float32) for r in res.results], axis=0)
```

```python
import math
import numpy as np
from contextlib import ExitStack
import concourse.bass as bass
import concourse.mybir as mybir
from concourse.bass_utils import run_bass_kernel_spmd

F32 = mybir.dt.float32
BF16 = mybir.dt.bfloat16
AF = mybir.ActivationFunctionType
ALU = mybir.AluOpType

D = 1024
S = 4096
DFF = 4096
T = 512
NT = S // T
NCH = S // 128
IN_COLS = 2568
ALPHA = 2.0 ** 0.25
LN_EPS = 1e-5
EPS_P = LN_EPS / (ALPHA * ALPHA)
NSLOT = 6
NSTREAM = 110


class Op:
    __slots__ = ("eng", "fn", "deps", "chan", "marked", "val", "idx")


class Prog:
    def __init__(self, nc, same_engine_sync=True):
        self.nc = nc
        self.ops = []
        self.last_writer = {}
        self.readers = {}
        self.chan_n = {}
        self.chan_bulk = {}
        self.same_engine_sync = same_engine_sync

    def add(self, eng, fn, reads=(), writes=(), chan=None, bulk=False):
        op = Op()
        op.eng = eng
        op.fn = fn
        op.chan = chan
        op.marked = False
        op.val = None
        op.idx = len(self.ops)
        deps = {}
        wset = set(writes)
        for r in reads:
            w = self.last_writer.get(r)
            if w is not None:
                deps[w.idx] = w
        for r in wset:
            w = self.last_writer.get(r)
            if w is not None:
                deps[w.idx] = w
            rd = self.readers.get(r)
            if rd:
                for o in rd.values():
                    deps[o.idx] = o
        for r in wset:
            self.last_writer[r] = op
            self.readers[r] = {}
        for r in reads:
            if r in wset:
                continue
            d = self.readers.setdefault(r, {})
            if chan is None:
                d[eng] = op
            else:
                d[("dma", op.idx)] = op
        deps.pop(op.idx, None)
        op.deps = list(deps.values())
        if chan is not None:
            self.chan_n[chan] = self.chan_n.get(chan, 0) + 1
            op.val = 16 * self.chan_n[chan]
            if bulk:
                self.chan_bulk[chan] = True
        self.ops.append(op)
        return op

    def _needs_wait(self, op, d):
        if d.chan is not None:
            if op.chan == d.chan and self.chan_bulk.get(d.chan):
                return False
            return True
        if d.eng == op.eng and op.chan is None:
            if op.eng == "pe":
                return False
            return self.same_engine_sync
        return True

    def emit(self, stack):
        nc = self.nc
        for op in self.ops:
            for d in op.deps:
                if d.chan is None and self._needs_wait(op, d):
                    d.marked = True
        cnt = {}
        for op in self.ops:
            if op.chan is None and op.marked:
                cnt[op.eng] = cnt.get(op.eng, 0) + 1
                op.val = cnt[op.eng]
        engs = sorted({op.eng for op in self.ops})
        sems = {}
        for e in engs:
            sems[e] = stack.enter_context(nc.semaphore("s_" + e))
        for i, c in enumerate(self.chan_n):
            sems[("c", c)] = stack.enter_context(nc.semaphore("c_%d" % i))
        chan_final = {c: 16 * n for c, n in self.chan_n.items()}
        per = {e: [op for op in self.ops if op.eng == e] for e in engs}
        block = stack.enter_context(nc.Block())

        def event(d):
            if d.chan is not None:
                if self.chan_bulk.get(d.chan):
                    return ("c", d.chan), chan_final[d.chan]
                return ("c", d.chan), d.val
            return d.eng, d.val

        def run(e, engobj):
            waited = {}
            for op in per[e]:
                need = {}
                for d in op.deps:
                    if not self._needs_wait(op, d):
                        continue
                    k, v = event(d)
                    if need.get(k, 0) < v:
                        need[k] = v
                for k, v in need.items():
                    if waited.get(k, 0) < v:
                        engobj.wait_ge(sems[k], v)
                        waited[k] = v
                ins = op.fn(engobj)
                if op.chan is not None:
                    ins.then_inc(sems[("c", op.chan)], 16)
                elif op.marked:
                    ins.then_inc(sems[e], 1)
            for c in self.chan_n:
                if any(o.chan == c for o in per[e]):
                    engobj.wait_ge(sems[("c", c)], chan_final[c])

        deco = {"pe": block.tensor, "act": block.scalar, "dve": block.vector,
                "pool": block.gpsimd, "sp": block.sync}
        for e in engs:
            def mk(e):
                def f(engobj):
                    run(e, engobj)
                return f
            deco[e](mk(e))


import os
XQ = os.environ.get("XQ", "sp")
LN_LNEXP = os.environ.get("LN_LNEXP", "0") == "1"
LN_POW = os.environ.get("LN_POW", "0") == "1"
FAST_RECIP = os.environ.get("FAST_RECIP", "0") == "1"


def RECIP(e, out, in_):
    if FAST_RECIP:
        return e.reciprocal_approx_fast(out=out, in_=in_)
    return e.reciprocal(out=out, in_=in_)


def build_program(same_engine_sync=True, n_tiles=NT, stage=99):
    nc = bass.Bass("TRN2", target_bir_lowering=False)
    stack = ExitStack()
    P = Prog(nc, same_engine_sync=same_engine_sync)

    def din(name, shape):
        return nc.dram_tensor(name, list(shape), F32, kind="ExternalInput").ap()

    x_d = din("x", [S, D])
    wg_d = [din("ffn1_w_gate", [1, D, DFF]), din("ffn2_w_gate", [1, D, DFF])]
    wu_d = [din("ffn1_w_up", [1, D, DFF]), din("ffn2_w_up", [1, D, DFF])]
    wd_d = [din("ffn1_w_down", [1, DFF, D]), din("ffn2_w_down", [1, DFF, D])]
    lng_d = [din("ln1_g", [1, D]), din("ln2_g", [1, D]), din("ln3_g", [1, D])]
    lnb_d = [din("ln1_b", [1, D]), din("ln2_b", [1, D]), din("ln3_b", [1, D])]
    win_d = din("w_in", [1, D, IN_COLS])
    bf_d = din("b_forget", [1, 8])
    smallp_d = din("smallp", [128, 32])
    wa_d = din("rg_wa", [1, 8, 64, 64])
    wx_d = din("rg_wx", [1, 8, 64, 64])
    wout_d = din("w_out", [1, D, D])
    out_d = nc.dram_tensor("out", [S, D], F32, kind="ExternalOutput").ap()
    scr = nc.dram_tensor("wscr", [NSTREAM, 128, 2048], BF16, kind="Internal").ap()

    def sb(name, shape, dt=F32):
        return stack.enter_context(nc.sbuf_tensor(name, list(shape), dt))

    kT = sb("kT", [128, 4, S], BF16)
    Vb = sb("Vb", [128, NCH, 768], BF16)
    ws = sb("ws", [128, NSLOT, 2048], BF16)
    xres = sb("xres", [128, 4, D])
    xT = sb("xT", [128, 8, T], BF16)
    hT = sb("hT", [128, 32, T], BF16)
    hT_flat = hT[:].rearrange("p a b -> p (a b)")
    qT = hT[:, 0:4, :]
    lx_all = hT_flat[:, 4 * 512:16 * 512].bitcast(F32).rearrange("p (c n) -> p c n", c=4)
    gel_all = hT_flat[:, 16 * 512:24 * 512].bitcast(F32).rearrange("p (c n) -> p c n", c=4)
    mixT = hT[:, 24:32, :]
    stg = hT_flat.bitcast(F32).rearrange("p (s n) -> p s n", s=4)

    def k_qT(c): return [("hT", c)]
    def k_lx(c): return [("hT", 4 + 3 * c + i) for i in range(3)]
    def k_gel(c): return [("hT", 16 + 2 * c), ("hT", 17 + 2 * c)]
    def k_mix(mc): return [("hT", 24 + mc)]
    def k_stg(s): return [("hT", 8 * s + i) for i in range(8)]
    K_XT = [("xT", tc) for tc in range(4)]

    lru_u = sb("lru_u", [128, T])
    lru_r = sb("lru_r", [128, T])
    lru_s = sb("lru_s", [128, T])
    lru_g = sb("lru_g", [128, T])
    lru_h = sb("lru_h", [128, T])
    Pt = sb("Pt", [128, 6, T], BF16)
    rinv = sb("rinv", [128, 1, T])
    lnp = sb("lnp", [128, 2, 2, D])
    sg = sb("sg", [128, 2, T])
    ident = sb("ident", [128, 128])
    tri = sb("tri", [128, 128])
    e127 = sb("e127", [128, 128])
    wabd = sb("wabd", [128, 4, 128], BF16)
    wxbd = sb("wxbd", [128, 4, 128], BF16)
    lru_u16 = sb("lru_u16", [128, T], BF16)
    wfg32 = sb("wfg32", [128, 8, 8])
    wfg = sb("wfg", [128, 8, 8], BF16)
    cumT = sb("cumT", [128, NCH, 8])
    biasT = sb("biasT", [128, NCH, 8])
    refbc = sb("refbc", [128, 8])
    Vs = sb("Vs", [128, 6, 128], BF16)
    smallp = sb("smallp_sb", [128, 32])
    lamc = sb("lamc", [128, 16])
    bfb = sb("bfb", [128, 8])
    fgt = sb("fgt", [128, 4, 2, 8])
    state = sb("state", [128, 4])
    halo = sb("halo", [128, 4, 3])
    stat6 = sb("stat6", [128, 4, 12])
    mv = sb("mv", [128, 4, 2])
    lnsm = sb("lnsm", [128, 4, 4])
    neghalf = sb("neghalf", [128, 1])

    pbig = stack.enter_context(nc.psum_tensor("pbig", [128, 8, 512], F32))
    banks = [pbig[:, b, :] for b in range(8)]

    def kb(b): return ("ps", b)

    def mm(out, lhsT, rhs, start, stop, reads, writes):
        P.add("pe", lambda e: e.matmul(out, lhsT=lhsT, rhs=rhs, start=start, stop=stop), reads=reads, writes=writes)

    def tr(out, in_, reads, writes):
        P.add("pe", lambda e: e.transpose(out=out, in_=in_, identity=ident[:]), reads=list(reads) + ["ident"], writes=writes)

    def act(out, in_, func, reads, writes, **kw):
        P.add("act", lambda e: e.activation(out=out, in_=in_, func=func, **kw), reads=reads, writes=writes)

    def dma(eng, out, in_, reads, writes, chan, bulk=False, **kw):
        P.add(eng, lambda e: e.dma_start(out=out, in_=in_, **kw), reads=reads, writes=writes, chan=chan, bulk=bulk)

    def vcopy(eng, out, in_, reads, writes):
        P.add(eng, lambda e: e.tensor_copy(out=out, in_=in_), reads=reads, writes=writes)

    def tt(eng, out, in0, in1, op, reads, writes):
        P.add(eng, lambda e: e.tensor_tensor(out=out, in0=in0, in1=in1, op=op), reads=reads, writes=writes)

    def ts(eng, out, in0, s1, s2, op0, op1, reads, writes):
        P.add(eng, lambda e: e.tensor_scalar(out=out, in0=in0, scalar1=s1, scalar2=s2, op0=op0, op1=op1), reads=reads, writes=writes)

    def stt(out, in0, scalar, in1, op0, op1, reads, writes):
        P.add("dve", lambda e: e.scalar_tensor_tensor(out=out, in0=in0, scalar=scalar, in1=in1, op0=op0, op1=op1), reads=reads, writes=writes)

    def memset(eng, ap, val, writes):
        P.add(eng, lambda e: e.memset(ap, val), writes=writes)

    for tc_ in range(4):
        dma(XQ, xres[:, tc_, :], x_d[tc_ * 128:(tc_ + 1) * 128, :], [], [("xres", tc_)], ("xld", tc_))

    memset("pool", ident[:], 1.0, ["ident"])
    P.add("pool", lambda e: e.affine_select(out=ident[:], in_=ident[:], pattern=[[1, 128]], compare_op=ALU.is_equal,
                                            fill=0.0, base=0, channel_multiplier=-1), reads=["ident"], writes=["ident"])
    memset("pool", tri[:], 1.0, ["tri"])
    P.add("pool", lambda e: e.affine_select(out=tri[:], in_=tri[:], pattern=[[1, 128]], compare_op=ALU.is_ge,
                                            fill=0.0, base=0, channel_multiplier=-1), reads=["tri"], writes=["tri"])
    memset("pool", e127[:], 1.0, ["e127"])
    P.add("pool", lambda e: e.affine_select(out=e127[:], in_=e127[:], pattern=[[0, 128]], compare_op=ALU.is_equal,
                                            fill=0.0, base=-127, channel_multiplier=1), reads=["e127"], writes=["e127"])
    wtmp = Pt[:, 0:4, :].rearrange("p a b -> p (a b)").bitcast(F32).rearrange("p (w c n) -> p w c n", w=2, c=4)
    memset("pool", wtmp, 0.0, [("P", k_) for k_ in range(4)])
    memset("dve", state[:], 0.0, ["state"])
    memset("dve", neghalf[:], -0.5, ["neghalf"])
    memset("dve", halo[:], 0.0, ["halo"])

    for c in range(4):
        for hf in range(2):
            dma("sp", wtmp[hf * 64:(hf + 1) * 64, 0, c, hf * 64:(hf + 1) * 64], wa_d[0, 2 * c + hf], [], [("P", k_) for k_ in range(4)], "par", bulk=True)
            dma("sp", wtmp[hf * 64:(hf + 1) * 64, 1, c, hf * 64:(hf + 1) * 64], wx_d[0, 2 * c + hf], [], [("P", k_) for k_ in range(4)], "par", bulk=True)
    dma("sp", smallp[:], smallp_d, [], ["smallp"], "par", bulk=True)
    dma("sp", bfb[:], bf_d[0].partition_broadcast(128), [], ["bfb"], "par", bulk=True)
    dma("sp", wfg32[:], win_d[0][:, 2560:2568].rearrange("(dc p) c -> p dc c", p=128), [], ["wfg32"], "par", bulk=True)
    vcopy("dve", wfg[:], wfg32[:], ["wfg32"], ["wfg"])
    vcopy("dve", wabd[:], wtmp[:, 0, :, :], [("P", k_) for k_ in range(4)], ["wabd"])
    vcopy("dve", wxbd[:], wtmp[:, 1, :, :], [("P", k_) for k_ in range(4)], ["wxbd"])
    act(lamc[:, 0:4], smallp[:, 28:32], AF.Exp, ["smallp"], ["lamc"], scale=-1.0)
    act(lamc[:, 0:4], lamc[:, 0:4], AF.Ln, ["lamc"], ["lamc"], bias=1.0)
    ts("dve", lamc[:, 4:8], lamc[:, 0:4], -16.0, None, ALU.mult, ALU.bypass, ["lamc"], ["lamc"])
    ts("dve", lamc[:, 0:4], lamc[:, 0:4], -8.0, None, ALU.mult, ALU.bypass, ["lamc"], ["lamc"])
    ts("dve", lamc[:, 8:16], smallp[:, 20:28], -1.0, None, ALU.mult, ALU.bypass, ["smallp"], ["lamc"])

    def src_cols(w2d, c0, ncols):
        return w2d[:, c0:c0 + ncols].rearrange("(dc p) c -> p dc c", p=128)

    def src_wd(w2d, fb, dh):
        return w2d[fb * 512:(fb + 1) * 512, dh * 512:(dh + 1) * 512].rearrange("(f p) c -> p f c", p=128)

    catalog = []
    for k in range(2):
        ent = []
        for fb2 in range(16):
            ent.append(src_cols(wg_d[k][0], fb2 * 256, 256))
            ent.append(src_cols(wu_d[k][0], fb2 * 256, 256))
        for dh in range(2):
            for fb in range(8):
                ent.append(src_wd(wd_d[k][0], fb, dh))
        if k == 0:
            catalog += ent
            for cb in range(10):
                catalog.append(src_cols(win_d[0], cb * 256, 256))
            for blk in range(4):
                catalog.append(src_cols(wout_d[0], blk * 256, 256))
        else:
            catalog += ent
    assert len(catalog) == NSTREAM

    NSTG = 3
    LA = 3
    cast_engs = ["dve", "act", "pool"]
    TOTAL = n_tiles * NSTREAM
    conv = [0] * NSTREAM

    NSTG4 = 4

    def stg_ap(s_):
        return Vb[:, 4 + 6 * s_:10 + 6 * s_, :].rearrange("p a b -> p (a b)")[:, 0:4096].bitcast(F32)

    def k_stg(s_):
        return [("V", 4 + 6 * s_ + q) for q in range(6)]

    pumped = [0]
    loaded = [0]
    pending_st = []

    def flush_stores(upto):
        while pending_st and pending_st[0][0] <= upto:
            _, e2, sl2 = pending_st.pop(0)
            dma("sp", scr[e2], ws[:, sl2, :], [("ws", sl2)], [("scr", e2)], ("scrst", sl2))

    def load_fp32(m):
        e = m % NSTREAM
        s_ = m % NSTG4
        src = catalog[e]
        a_ = src.shape[1]
        dma("sp", stg_ap(s_).rearrange("p (a b) -> p a b", a=a_), src, [], k_stg(s_), ("stg", s_))

    def pump(m):
        tile_m, e = divmod(m, NSTREAM)
        sl = m % NSLOT
        if tile_m > conv[e]:
            dma("sp", ws[:, sl, :], scr[e], [("scr", e)], [("ws", sl)], ("ws", sl))
        else:
            while loaded[0] <= min(m + 1, NSTREAM - 1):
                load_fp32(loaded[0])
                loaded[0] += 1
            s_ = m % NSTG4
            copy_any("act", ws[:, sl, :], stg_ap(s_), k_stg(s_), [("ws", sl)])
            if tile_m == conv[e]:
                dma("act", scr[e], ws[:, sl, :], [("ws", sl)], [("scr", e)], ("scrst", sl))

    stream_pos = [0]

    def next_slot(tile_n):
        n = stream_pos[0]
        stream_pos[0] += 1
        while pumped[0] <= min(n + LA, TOTAL - 1):
            pump(pumped[0])
            pumped[0] += 1
        if n == TOTAL - 1:
            flush_stores(TOTAL)
        return n % NSLOT

    ln_phase = [0]

    def ln_params(which):
        n = ln_phase[0]
        ln_phase[0] += 1
        r = n % 2
        dma("sp", lnp[:, r, 0, :], lng_d[which][0].partition_broadcast(128), [], [("lnp", r)], ("lnp", r))
        dma("sp", lnp[:, r, 1, :], lnb_d[which][0].partition_broadcast(128), [], [("lnp", r)], ("lnp", r))
        return r

    rr = {"evac": 0}

    def evac_engine():
        rr["evac"] += 1
        return "act" if rr["evac"] % 2 else "dve"

    def copy_any(eng, out, in_, reads, writes):
        if eng == "act":
            P.add("act", lambda e: e.copy(out=out, in_=in_), reads=reads, writes=writes)
        else:
            vcopy(eng, out, in_, reads, writes)

    def make_xT():
        for tc in range(4):
            for g in range(2):
                b = (tc * 2 + g) % 8
                for j in range(4):
                    dc = 4 * g + j
                    tr(banks[b][:, j * 128:(j + 1) * 128], xres[:, tc, dc * 128:(dc + 1) * 128], [("xres", tc)], [kb(b)])
                copy_any(evac_engine(), xT[:, 4 * g:4 * g + 4, tc * 128:(tc + 1) * 128],
                         banks[b][:].rearrange("p (j n) -> p j n", j=4), [kb(b)], [("xT", tc)])

    sg_flat = sg[:].rearrange("p a b -> p (a b)")
    qzB = sg_flat.bitcast(BF16).rearrange("p (h n) -> p h n", h=4)

    def qz(h):
        return qT[:, h, :] if h < 4 else qzB[:, h - 4, :]

    def k_qz(h):
        return [("hT", h)] if h < 4 else [("sg", (h - 4) // 2)]

    def prefetch_xT(tile_n, tc):
        r0 = tile_n * T + tc * 128
        ksg = [("sg", 0), ("sg", 1)]
        dma("sp", sg_flat, x_d[r0:r0 + 128, :], [], ksg, "xpf")
        for g in range(2):
            b = (tc % 2) * 2 + g
            for j in range(4):
                dc = 4 * g + j
                tr(banks[b][:, j * 128:(j + 1) * 128], sg_flat[:, dc * 128:(dc + 1) * 128], ksg, [kb(b)])
            copy_any(evac_engine(), xT[:, 4 * g:4 * g + 4, tc * 128:(tc + 1) * 128],
                     banks[b][:].rearrange("p (j n) -> p j n", j=4), [kb(b)], [("xT", tc)])

    def ln_stats(tc, hh):
        P.add("dve", lambda e: e.bn_stats(out=stat6[:, tc, hh * 6:(hh + 1) * 6], in_=xres[:, tc, hh * 512:(hh + 1) * 512]),
              reads=[("xres", tc)], writes=[("stat6", tc, hh)])

    def layer_norm(tc, which_r, final_tile=None):
        kx = ("xres", tc)
        P.add("dve", lambda e: e.bn_aggr(out=mv[:, tc, :], in_=stat6[:, tc, :]), reads=[("stat6", tc, 0), ("stat6", tc, 1)], writes=[("mv", tc)])
        if LN_POW:
            ts("pool", lnsm[:, tc, 0:1], mv[:, tc, 1:2], EPS_P, 1.0, ALU.add, ALU.mult, [("mv", tc)], [("lnsm", tc)])
            tt("pool", lnsm[:, tc, 1:2], lnsm[:, tc, 0:1], neghalf[:, 0:1], ALU.pow, [("lnsm", tc), "neghalf"], [("lnsm", tc)])
        elif LN_LNEXP:
            act(lnsm[:, tc, 0:1], mv[:, tc, 1:2], AF.Ln, [("mv", tc)], [("lnsm", tc)], bias=EPS_P, scale=1.0)
            act(lnsm[:, tc, 1:2], lnsm[:, tc, 0:1], AF.Exp, [("lnsm", tc)], [("lnsm", tc)], scale=-0.5)
        else:
            act(lnsm[:, tc, 0:1], mv[:, tc, 1:2], AF.Sqrt, [("mv", tc)], [("lnsm", tc)], bias=EPS_P, scale=1.0)
            P.add("dve", lambda e: e.reciprocal(out=lnsm[:, tc, 1:2], in_=lnsm[:, tc, 0:1]), reads=[("lnsm", tc)], writes=[("lnsm", tc)])
        ts("dve", lnsm[:, tc, 2:3], mv[:, tc, 0:1], -1.0, lnsm[:, tc, 1:2], ALU.mult, ALU.mult, [("mv", tc), ("lnsm", tc)], [("lnsm", tc)])
        act(xres[:, tc, :], xres[:, tc, :], AF.Identity, [kx, ("lnsm", tc)], [kx], scale=lnsm[:, tc, 1:2], bias=lnsm[:, tc, 2:3])
        tt("dve", xres[:, tc, :], xres[:, tc, :], lnp[:, which_r, 0, :], ALU.mult, [kx, ("lnp", which_r)], [kx])
        tt("pool", xres[:, tc, :], xres[:, tc, :], lnp[:, which_r, 1, :], ALU.add, [kx, ("lnp", which_r)], [kx])

    def ln_tail(which_r, M, evac, do_xT):
        def A(tc):
            evac(tc)
            P.add("dve", lambda e: e.bn_aggr(out=mv[:, tc, :], in_=stat6[:, tc, :]), reads=[("stat6", tc, 0), ("stat6", tc, 1)], writes=[("mv", tc)])

        def B(tc):
            act(lnsm[:, tc, 0:1], mv[:, tc, 1:2], AF.Sqrt, [("mv", tc)], [("lnsm", tc)], bias=EPS_P, scale=1.0)

        def C(tc):
            P.add("dve", lambda e: e.reciprocal(out=lnsm[:, tc, 1:2], in_=lnsm[:, tc, 0:1]), reads=[("lnsm", tc)], writes=[("lnsm", tc)])
            ts("dve", lnsm[:, tc, 2:3], mv[:, tc, 0:1], -1.0, lnsm[:, tc, 1:2], ALU.mult, ALU.mult, [("mv", tc), ("lnsm", tc)], [("lnsm", tc)])

        def D(tc):
            act(xres[:, tc, :], xres[:, tc, :], AF.Identity, [("xres", tc), ("lnsm", tc)], [("xres", tc)], scale=lnsm[:, tc, 1:2], bias=lnsm[:, tc, 2:3])

        def E(tc):
            tt("dve", xres[:, tc, :], xres[:, tc, :], lnp[:, which_r, 0, :], ALU.mult, [("xres", tc), ("lnp", which_r)], [("xres", tc)])

        def F(tc):
            tt("pool", xres[:, tc, :], xres[:, tc, :], lnp[:, which_r, 1, :], ALU.add, [("xres", tc), ("lnp", which_r)], [("xres", tc)])

        def G(tc):
            if not do_xT:
                return
            for g in range(2):
                b = 4 + 2 * (tc % 2) + g
                for j in range(4):
                    dc = 4 * g + j
                    tr(banks[b][:, j * 128:(j + 1) * 128], xres[:, tc, dc * 128:(dc + 1) * 128], [("xres", tc)], [kb(b)])
                copy_any("act", xT[:, 4 * g:4 * g + 4, tc * 128:(tc + 1) * 128],
                         banks[b][:].rearrange("p (j n) -> p j n", j=4), [kb(b)], [("xT", tc)])

        def Mf(tc):
            if M is not None:
                M(tc)

        st = {"M": Mf, "A": A, "B": B, "C": C, "D": D, "E": E, "F": F, "G": G}
        order = "M0 A0 B0 M1 A1 B1 C0 D0 M2 A2 B2 C1 D1 E0 F0 M3 A3 B3 C2 D2 E1 F1 G0 C3 D3 E2 F2 G1 E3 F3 G2 G3"
        for tok in order.split():
            st[tok[0]](int(tok[1]))

    def ffn(tile_n, which_ln, c1, prefetch_tile=None, hook=None):
        for fb2 in range(16):
            if hook is not None and fb2 == 4:
                hook()
            sg_ = next_slot(tile_n)
            su_ = next_slot(tile_n)
            for fcl in range(2):
                fc = 2 * fb2 + fcl
                par = fc % 2
                bg, bu = 2 * par, 2 * par + 1
                for dc in range(8):
                    mm(banks[bg][:], ws[:, sg_, dc * 256 + fcl * 128: dc * 256 + (fcl + 1) * 128], xT[:, dc, :], dc == 0, dc == 7,
                       [("ws", sg_)] + K_XT, [kb(bg)])
                for dc in range(8):
                    mm(banks[bu][:], ws[:, su_, dc * 256 + fcl * 128: dc * 256 + (fcl + 1) * 128], xT[:, dc, :], dc == 0, dc == 7,
                       [("ws", su_)] + K_XT, [kb(bu)])
                act(sg[:, par, :], banks[bg][:], AF.Silu, [kb(bg)], [("sg", par)])
                tt("dve", hT[:, fc, :], sg[:, par, :], banks[bu][:], ALU.mult, [("sg", par), kb(bu)], [("hT", fc)])
        r = ln_params(which_ln)
        for dh in range(2):
            bb = [4, 5, 6, 7] if dh == 0 else [0, 1, 2, 3]
            nfb = 8 if dh == 0 else 6
            for fb in range(nfb):
                sl = next_slot(tile_n)
                for fcl in range(4):
                    fc = 4 * fb + fcl
                    for tc in range(4):
                        mm(banks[bb[tc]][:], hT[:, fc, tc * 128:(tc + 1) * 128], ws[:, sl, fcl * 512:(fcl + 1) * 512], fc == 0, fc == 31,
                           [("hT", fc), ("ws", sl)], [kb(bb[tc])])
                if prefetch_tile is not None and dh == 0 and fb % 2 == 0:
                    prefetch_xT(prefetch_tile, fb // 2)
            if dh == 0:
                for tc in range(4):
                    stt(xres[:, tc, 0:512], banks[bb[tc]][:], c1, xres[:, tc, 0:512], ALU.mult, ALU.add,
                        [kb(bb[tc]), ("xres", tc)], [("xres", tc)])
                    ln_stats(tc, 0)
            else:
                tail_slots = [next_slot(tile_n), next_slot(tile_n)]

                def M(tc, bb=bb, tail_slots=tail_slots):
                    for q_, sl_ in enumerate(tail_slots):
                        for fcl in range(4):
                            fc = 4 * (6 + q_) + fcl
                            mm(banks[bb[tc]][:], hT[:, fc, tc * 128:(tc + 1) * 128], ws[:, sl_, fcl * 512:(fcl + 1) * 512], False, fc == 31,
                               [("hT", fc), ("ws", sl_)], [kb(bb[tc])])

                def evac(tc, bb=bb):
                    stt(xres[:, tc, 512:1024], banks[bb[tc]][:], c1, xres[:, tc, 512:1024], ALU.mult, ALU.add,
                        [kb(bb[tc]), ("xres", tc)], [("xres", tc)])
                    ln_stats(tc, 1)

                ln_tail(r, M, evac, do_xT=(which_ln == 0))

    def mixer(i):
        bank_rr = [0]

        def nb():
            b = bank_rr[0] % 8
            bank_rr[0] += 1
            return b

        P.add("pool", lambda e: e.memset(qT[:, 0:4, :], 0.0), writes=[("hT", c_) for c_ in range(4)])
        P.add("pool", lambda e: e.memset(qzB[:], 0.0), writes=[("sg", 0), ("sg", 1)])
        ones_dst = Vb[:, 4 * i:4 * i + 4, :].rearrange("p j (q t) -> p j q t", q=4)[:, :, :, 64:128]
        P.add("pool", lambda e: e.memset(ones_dst, 1.0), writes=[("V", 4 * i + tc) for tc in range(4)])

        def proj_chunk(sl, half, cc):
            b = nb()
            for dc in range(8):
                mm(banks[b][:], ws[:, sl, dc * 256 + half * 128: dc * 256 + (half + 1) * 128], xT[:, dc, :], dc == 0, dc == 7,
                   [("ws", sl)] + K_XT, [kb(b)])
            if cc < 4:
                for e2 in range(2):
                    h_ = 2 * cc + e2
                    P.add("act", (lambda h_=h_, e2=e2, b=b: lambda e: e.activation(out=qz(h_)[e2 * 64:(e2 + 1) * 64, :], in_=banks[b][e2 * 64:(e2 + 1) * 64, :],
                                                                             func=AF.Copy, scale=0.125))(),
                          reads=[kb(b)], writes=k_qz(h_))
            elif cc < 8:
                vcopy("dve", kT[:, cc - 4, i * T:(i + 1) * T], banks[b][:], [kb(b)], [("kT", cc - 4, i)])
            elif cc < 16:
                c = cc - 12
                copy_any(evac_engine(), lx_all[:, c, 3:3 + T], banks[b][:], [kb(b)], k_lx(c))
            else:
                c = cc - 16
                act(gel_all[:, c, :], banks[b][:], AF.Gelu_apprx_tanh, [kb(b)], k_gel(c))

        for cb in range(4):
            sl = next_slot(i)
            for half in range(2):
                proj_chunk(sl, half, 2 * cb + half)
        v_slots = [next_slot(i), next_slot(i)]
        for tc in range(4):
            j = 4 * i + tc
            b = nb()
            for vi, vs in enumerate(v_slots):
                for dc in range(8):
                    mm(banks[b][:, vi * 256:(vi + 1) * 256], xT[:, dc, tc * 128:(tc + 1) * 128], ws[:, vs, dc * 256:(dc + 1) * 256],
                       dc == 0, dc == 7, [("ws", vs), ("xT", tc)], [kb(b)])
            src = banks[b][:].rearrange("p (q e d) -> p q e d", q=4, e=2)
            dst = Vb[:, j, :].rearrange("p (q t) -> p q t", q=4)
            vcopy("dve", dst[:, :, 0:64], src[:, :, 0, :], [kb(b)], [("V", j)])
            P.add("act", (lambda dst=dst, src=src: lambda e: e.copy(out=dst[:, :, 128:192], in_=src[:, :, 1, :]))(), reads=[kb(b)], writes=[("V", j)])
            b2 = nb()
            for dc in range(8):
                mm(banks[b2][:, 0:8], xT[:, dc, tc * 128:(tc + 1) * 128], wfg[:, dc, :], dc == 0, dc == 7, [("xT", tc), "wfg"], [kb(b2)])
            tt("dve", fgt[:, tc, 0, :], banks[b2][:, 0:8], bfb[:], ALU.add, [kb(b2), "bfb"], [("fgt0", tc)])
            act(fgt[:, tc, 1, :], fgt[:, tc, 0, :], AF.Exp, [("fgt0", tc)], [("fgt1", tc)], scale=-1.0)
            act(fgt[:, tc, 1, :], fgt[:, tc, 1, :], AF.Ln, [("fgt1", tc)], [("fgt1", tc)], bias=1.0)
        def cumsum_step(tc):
            j = 4 * i + tc
            b3 = nb()
            mm(banks[b3][:, 0:8], tri[:], fgt[:, tc, 1, :], True, j == 0, ["tri", ("fgt1", tc)], [kb(b3)])
            if j > 0:
                mm(banks[b3][:, 0:8], e127[:], cumT[:, j - 1, :], False, True, ["e127", ("cumT", j - 1)], [kb(b3)])
            vcopy("dve", cumT[:, j, :], banks[b3][:, 0:8], [kb(b3)], [("cumT", j)])

        for cb in range(6, 10):
            sl = next_slot(i)
            for half in range(2):
                proj_chunk(sl, half, 2 * cb + half)
            cumsum_step(cb - 6)

        nj = 4 * i + 4
        bref = nb()
        mm(banks[bref][:, 0:8], e127[:], cumT[:, 4 * i + 1, :], True, True, ["e127", ("cumT", 4 * i + 1)], [kb(bref)])
        vcopy("dve", refbc[:], banks[bref][:, 0:8], [kb(bref)], ["refbc"])
        tt("dve", biasT[:, 0:nj, :], cumT[:, 0:nj, :], refbc[:].unsqueeze(1).to_broadcast([128, nj, 8]), ALU.subtract,
           [("cumT", j) for j in range(nj)] + ["refbc"], ["biasT"])

        s_rr = [0]

        def lru_conv(c):
            vcopy("pool", lx_all[:, c, 0:3], halo[:, c, :], ["halo"], k_lx(c))
            ts("dve", lru_u[:], lx_all[:, c, 0:T], smallp[:, c * 4:c * 4 + 1], smallp[:, 16 + c:17 + c], ALU.mult, ALU.add,
               k_lx(c) + ["smallp"], ["lru_u"])
            for k in range(1, 4):
                stt(lru_u[:], lx_all[:, c, k:k + T], smallp[:, c * 4 + k:c * 4 + k + 1], lru_u[:], ALU.mult, ALU.add,
                    k_lx(c) + ["smallp", "lru_u"], ["lru_u"])
            vcopy("pool", halo[:, c, :], lx_all[:, c, T:T + 3], k_lx(c), ["halo"])
            vcopy("dve", lru_u16[:], lru_u[:], ["lru_u"], ["lru_u16"])

        group_taker = [None]

        def lru_gates(c):
            g_ = group_taker[0]()
            ba_ = 2 * g_
            bx_ = 2 * g_ + 1
            mm(banks[ba_][:], wabd[:, c, :], lru_u16[:], True, True, ["wabd", "lru_u16"], [kb(ba_)])
            mm(banks[bx_][:], wxbd[:, c, :], lru_u16[:], True, True, ["wxbd", "lru_u16"], [kb(bx_)])
            act(lru_r[:], banks[ba_][:], AF.Exp, [kb(ba_), "lamc"], ["lru_r"], scale=-1.0, bias=lamc[:, 8 + c:9 + c])
            act(lru_g[:], banks[bx_][:], AF.Exp, [kb(bx_), "lamc"], ["lru_g"], scale=-1.0, bias=lamc[:, 12 + c:13 + c])
            ts("pool", lru_r[:], lru_r[:], 1.0, 1.0, ALU.add, ALU.mult, ["lru_r"], ["lru_r"])
            P.add("dve", lambda e: RECIP(e, lru_r[:], lru_r[:]), reads=["lru_r"], writes=["lru_r"])
            ts("pool", lru_g[:], lru_g[:], 1.0, 1.0, ALU.add, ALU.mult, ["lru_g"], ["lru_g"])
            P.add("dve", lambda e: RECIP(e, lru_g[:], lru_g[:]), reads=["lru_g"], writes=["lru_g"])
            tt("dve", lru_g[:], lru_g[:], lru_u[:], ALU.mult, ["lru_g", "lru_u"], ["lru_g"])

        def lru_rest(c):
            act(lru_s[:], lru_r[:], AF.Exp, ["lru_r", "lamc"], ["lru_s"], scale=lamc[:, 4 + c:5 + c])
            act(lru_r[:], lru_r[:], AF.Exp, ["lru_r", "lamc"], ["lru_r"], scale=lamc[:, c:c + 1])
            act(lru_s[:], lru_s[:], AF.Ln, ["lru_s"], ["lru_s"], scale=-1.0, bias=1.0)
            act(lru_s[:], lru_s[:], AF.Exp, ["lru_s"], ["lru_s"], scale=0.5)
            tt("pool", lru_s[:], lru_s[:], lru_g[:], ALU.mult, ["lru_s", "lru_g"], ["lru_s"])
            P.add("dve", (lambda c=c: lambda e: e.tensor_tensor_scan(out=lru_h[:], data0=lru_r[:], data1=lru_s[:], initial=state[:, c:c + 1],
                                                                      op0=ALU.mult, op1=ALU.add))(),
                  reads=["lru_r", "lru_s", "state"], writes=["lru_h"])
            vcopy("dve", state[:, c:c + 1], lru_h[:, T - 1:T], ["lru_h"], ["state"])
            tt("dve", mixT[:, 4 + c, :], gel_all[:, c, :], lru_h[:], ALU.mult, k_gel(c) + ["lru_h"], k_mix(4 + c))

        act(biasT[:, 0:nj, :], biasT[:, 0:nj, :], AF.Exp, ["biasT"], ["biasT"])
        NG = 3
        held = set()
        last_g = [NG - 1]

        def take_group():
            for d_ in range(1, NG + 1):
                g = (last_g[0] + d_) % NG
                if g not in held:
                    last_g[0] = g
                    return g
            raise AssertionError("no free score group")

        group_taker[0] = take_group

        def cols_of(j):
            jj = j - 4 * i
            return (jj * 128 if jj > 0 else 0), T

        def issue_S(u):
            h, kind, j, first, last = u
            kc = h // 2
            vcol = kc * 192 + (h % 2) * 64
            if first and h % 2 == 0:
                lru_conv(h // 2)
            g = take_group()
            held.add(g)
            if kind == "pair":
                for q_ in range(2):
                    bk_ = 2 * g + q_
                    mm(banks[bk_][:, :], kT[:, kc, (j + q_) * 128:(j + q_ + 1) * 128], qz(h)[:, :], True, True,
                       [("kT", kc, (j + q_) // 4)] + k_qz(h), [kb(bk_)])
                act(Pt[:, 2 * g:2 * g + 2, :], pbig[:, 2 * g:2 * g + 2, :], AF.Exp, [kb(2 * g), kb(2 * g + 1)], [("P", 2 * g), ("P", 2 * g + 1)])
            else:
                bk_ = 2 * g
                c0, c1_ = cols_of(j)
                mm(banks[bk_][:, c0:c1_], kT[:, kc, j * 128:(j + 1) * 128], qz(h)[:, c0:c1_], True, True,
                   [("kT", kc, j // 4)] + k_qz(h), [kb(bk_)])
                act(Pt[:, bk_, c0:c1_], banks[bk_][:, c0:c1_], AF.Exp, [kb(bk_)], [("P", bk_)])
                P.add("pool", (lambda k=bk_, c0=c0: lambda e: e.affine_select(out=Pt[:, k, c0:c0 + 128], in_=Pt[:, k, c0:c0 + 128], pattern=[[1, 128]],
                                                                               compare_op=ALU.is_ge, fill=0.0, base=0, channel_multiplier=-1))(),
                      reads=[("P", bk_)], writes=[("P", bk_)])
            for q_, jx in enumerate([j, j + 1] if kind == "pair" else [j]):
                r_ = 2 * g + q_
                ts("pool", Vs[:, r_, :], Vb[:, jx, vcol:vcol + 128], biasT[:, jx, h:h + 1], 1.0, ALU.mult, ALU.mult,
                   [("V", jx), "biasT"], [("Vs", r_)])
            return g

        def issue_PV(u, g):
            h, kind, j, first, last = u
            kc = h // 2
            ob = 6 + (h % 2)
            js = [j, j + 1] if kind == "pair" else [j]
            for q_, jx in enumerate(js):
                bk_ = 2 * g + q_
                c0, c1_ = cols_of(jx) if kind == "single" else (0, T)
                mm(banks[ob][:, c0:c1_], Vs[:, bk_, :], Pt[:, bk_, c0:c1_], first and q_ == 0, last and q_ == len(js) - 1,
                   [("Vs", bk_), ("P", bk_)], [kb(ob)])
            held.discard(g)
            if last:
                if h % 2 == 0:
                    P.add("dve", (lambda ob=ob: lambda e: RECIP(e, rinv[64:128, 0, :], banks[ob][64:128, :]))(), reads=[kb(ob)], writes=[("rinv", 0)])
                    tt("dve", mixT[0:64, kc, :], banks[ob][0:64, :], rinv[64:128, 0, :], ALU.mult, [kb(ob), ("rinv", 0)], k_mix(kc))
                    lru_gates(h // 2)
                else:
                    P.add("dve", (lambda ob=ob: lambda e: RECIP(e, rinv[0:64, 0, :], banks[ob][0:64, :]))(), reads=[kb(ob)], writes=[("rinv", 0)])
                    tt("dve", mixT[64:128, kc, :], banks[ob][64:128, :], rinv[0:64, 0, :], ALU.mult, [kb(ob), ("rinv", 0)], k_mix(kc))
                    lru_rest(h // 2)

        all_units = []
        for h in range(8):
            uh = [("pair", j) for j in range(0, 4 * i, 2)] + [("single", j) for j in range(4 * i, nj)]
            for q_, (kind, j) in enumerate(uh):
                all_units.append((h, kind, j, q_ == 0, q_ == len(uh) - 1))
        pend = []
        nxt = 0
        while nxt < min(2, len(all_units)):
            pend.append((all_units[nxt], issue_S(all_units[nxt])))
            nxt += 1
        while pend:
            u, g = pend.pop(0)
            if nxt < len(all_units):
                pend.append((all_units[nxt], issue_S(all_units[nxt])))
                nxt += 1
            issue_PV(u, g)

        r = ln_params(1)
        bb = [0, 1, 2, 3]
        for dh in range(2):
            for bk in range(2):
                sl = next_slot(i)
                if dh == 1 and bk == 1:
                    def M(tc, sl=sl):
                        for mc in range(8):
                            mm(banks[bb[tc]][:, 256:512], mixT[:, mc, tc * 128:(tc + 1) * 128], ws[:, sl, mc * 256:(mc + 1) * 256],
                               mc == 0, mc == 7, k_mix(mc) + [("ws", sl)], [kb(bb[tc])])

                    def evac(tc):
                        stt(xres[:, tc, 512:1024], banks[bb[tc]][:], 1.0 / ALPHA, xres[:, tc, 512:1024], ALU.mult, ALU.add,
                            [kb(bb[tc]), ("xres", tc)], [("xres", tc)])
                        ln_stats(tc, 1)

                    ln_tail(r, M, evac, do_xT=True)
                    continue
                for mc in range(8):
                    for tc in range(4):
                        mm(banks[bb[tc]][:, bk * 256:(bk + 1) * 256], mixT[:, mc, tc * 128:(tc + 1) * 128], ws[:, sl, mc * 256:(mc + 1) * 256],
                           mc == 0, mc == 7, k_mix(mc) + [("ws", sl)], [kb(bb[tc])])
            if dh == 0:
                for tc in range(4):
                    stt(xres[:, tc, 0:512], banks[bb[tc]][:], 1.0 / ALPHA, xres[:, tc, 0:512], ALU.mult, ALU.add,
                        [kb(bb[tc]), ("xres", tc)], [("xres", tc)])
                    ln_stats(tc, 0)

    def load_x(i):
        for tc in range(4):
            r0 = i * T + tc * 128
            dma(XQ, xres[:, tc, :], x_d[r0:r0 + 128, :], [], [("xres", tc)], ("xld", tc))

    def store_out(i):
        for tc in range(4):
            r0 = i * T + tc * 128
            dma(XQ, out_d[r0:r0 + 128, :], xres[:, tc, :], [("xres", tc)], [], ("ost", tc))

    pending = []
    for i in range(n_tiles):
        hook = None
        if pending:
            todo = list(pending)
            pending = []

            def hook(todo=todo):
                for f in todo:
                    f()
        if stage >= 1 and not (i > 0 and stage >= 4):
            make_xT()
        if stage >= 2:
            ffn(i, 0, 0.5 / ALPHA, hook=hook)
        elif hook is not None:
            hook()
        if stage >= 3:
            mixer(i)
        if stage >= 4:
            ffn(i, 2, 0.5 / ALPHA, prefetch_tile=(i + 1 if i + 1 < n_tiles else None))
        if i == n_tiles - 1:
            store_out(i)
        else:
            pending = [(lambda i=i: store_out(i)), (lambda i=i: load_x(i + 1))]

    P.emit(stack)
    stack.close()
    return nc


_CACHE = {}


def _pack_small(conv_w, conv_b, rg_ba, rg_bx, lru_lambda):
    sp = np.zeros((128, 32), np.float32)
    cw = np.asarray(conv_w, np.float32)[0]
    for c in range(4):
        for k in range(4):
            sp[:, c * 4 + k] = cw[k, c * 128:(c + 1) * 128]
    sp[:, 16:20] = np.asarray(conv_b, np.float32)[0].reshape(4, 128).T
    sp[:, 20:24] = np.asarray(rg_ba, np.float32)[0].reshape(4, 128).T
    sp[:, 24:28] = np.asarray(rg_bx, np.float32)[0].reshape(4, 128).T
    sp[:, 28:32] = np.asarray(lru_lambda, np.float32)[0].reshape(4, 128).T
    return sp


def kernel(x, ffn1_w_gate, ffn1_w_up, ffn1_w_down, ln1_g, ln1_b, w_in, b_forget,
           conv_w, conv_b, rg_wa, rg_ba, rg_wx, rg_bx, lru_lambda, w_out,
           ln2_g, ln2_b, ffn2_w_gate, ffn2_w_up, ffn2_w_down, ln3_g, ln3_b):
    if "nc" not in _CACHE:
        _CACHE["nc"] = build_program()
    nc = _CACHE["nc"]
    f = lambda a: np.ascontiguousarray(np.asarray(a, dtype=np.float32))
    shared = {
        "ffn1_w_gate": f(ffn1_w_gate), "ffn1_w_up": f(ffn1_w_up), "ffn1_w_down": f(ffn1_w_down),
        "ffn2_w_gate": f(ffn2_w_gate), "ffn2_w_up": f(ffn2_w_up), "ffn2_w_down": f(ffn2_w_down),
        "ln1_g": f(ln1_g), "ln1_b": f(ln1_b), "ln2_g": f(ln2_g), "ln2_b": f(ln2_b), "ln3_g": f(ln3_g), "ln3_b": f(ln3_b),
        "w_in": f(w_in), "b_forget": f(b_forget), "rg_wa": f(rg_wa), "rg_wx": f(rg_wx), "w_out": f(w_out),
        "smallp": _pack_small(conv_w, conv_b, rg_ba, rg_bx, lru_lambda),
    }
    xs = f(x)
    in_maps = []
    for b in range(8):
        m = dict(shared)
        m["x"] = xs[b]
        in_maps.append(m)
    res = run_bass_kernel_spmd(nc, in_maps, core_ids=list(range(8)))
    return np.stack([np.asarray(r["out"], dtype=np.float32) for r in res.results], axis=0)
```
